# Optimizing a Trainium2 kernel written in Bass

```python
import math
import jax, jax.numpy as jnp
from jax import lax
import numpy as np

D_MODEL = 2048
BATCH = 16
SEQ = 2048
DEPTH = 4

N_CONV_LAYERS = DEPTH // 2
N_ATTN_LAYERS = DEPTH - N_CONV_LAYERS
CONV_WIDTH = 31
D_FF = 4 * D_MODEL
HEAD_DIM = 128
N_HEADS = D_MODEL // HEAD_DIM
N_KV_HEADS = 4
GROUP = N_HEADS // N_KV_HEADS
ROPE_DIM = HEAD_DIM // 4
ROPE_THETA = 500000.0
CMP_BLOCK = 32
CMP_STRIDE = 16
CMP_HIDDEN = 4 * HEAD_DIM
SEL_BLOCK = 64
SEL_TOP_N = 16
WINDOW = 512
WIN_QBLOCK = 128
SEL_QCHUNK = 16
N_BRANCH = 3
N_KV_TENSORS = 6
FORCE_BONUS = 1000.0
EPS = 1e-6
NEG = -1e30

kernel_name = "yoco_conformer_nsa_hybrid"


def rmsnorm(x, g):
    xf = x.astype(jnp.float32)
    y = xf * lax.rsqrt(jnp.mean(xf * xf, axis=-1, keepdims=True) + EPS) * g.astype(jnp.float32)
    return y.astype(x.dtype)


def layernorm(x, g, b):
    xf = x.astype(jnp.float32)
    mu = jnp.mean(xf, axis=-1, keepdims=True)
    var = jnp.mean(jnp.square(xf - mu), axis=-1, keepdims=True)
    y = (xf - mu) * lax.rsqrt(var + EPS) * g.astype(jnp.float32) + b.astype(jnp.float32)
    return y.astype(x.dtype)


def rope_partial(x, pos):
    half = ROPE_DIM // 2
    inv = ROPE_THETA ** (-jnp.arange(0, ROPE_DIM, 2, dtype=jnp.float32) / ROPE_DIM)
    ang = pos.astype(jnp.float32)[:, None] * inv[None, :]
    cos, sin = jnp.cos(ang), jnp.sin(ang)
    xf = x.astype(jnp.float32)
    x1, x2 = xf[..., :half], xf[..., half:ROPE_DIM]
    out = jnp.concatenate([x1 * cos - x2 * sin, x2 * cos + x1 * sin, xf[..., ROPE_DIM:]], axis=-1)
    return out.astype(x.dtype)


def masked_softmax(s, mask):
    s = jnp.where(mask, s.astype(jnp.float32), NEG)
    p = jax.nn.softmax(s, axis=-1)
    return jnp.where(mask, p, 0.0)


def conv_module(h, w_pw1, b_pw1, w_dw, b_dw, ln_g, ln_b, w_pw2, b_pw2):
    u = h @ w_pw1 + b_pw1
    a, gate = jnp.split(u, 2, axis=-1)
    u = a * jax.nn.sigmoid(gate)
    u = lax.conv_general_dilated(u, w_dw[:, None, :], window_strides=(1,),
                                 padding=[(CONV_WIDTH - 1, 0)],
                                 dimension_numbers=("NWC", "WIO", "NWC"),
                                 feature_group_count=D_MODEL) + b_dw
    u = jax.nn.silu(layernorm(u, ln_g, ln_b))
    return u @ w_pw2 + b_pw2


def sq_relu_mlp(h, w_up, w_down):
    return jnp.square(jax.nn.relu(h @ w_up)) @ w_down


def compress_blocks(kb, w1, w2):
    B, G, NC = kb.shape[:3]
    flat = kb.reshape(B, G, NC, CMP_BLOCK * HEAD_DIM)
    return jax.nn.silu(flat @ w1) @ w2


def shared_kv(s, w_kv, cmp_pe_k, cmp_pe_v, cmp_k_w1, cmp_k_w2, cmp_v_w1, cmp_v_w2):
    B, S, _ = s.shape
    pos = jnp.arange(S)
    kv = (s @ w_kv).reshape(B, S, N_KV_TENSORS, N_KV_HEADS, HEAD_DIM).transpose(2, 0, 3, 1, 4)
    k_c, v_c, k_s, v_s, k_w, v_w = [kv[i] for i in range(N_KV_TENSORS)]
    n_cmp = (S - CMP_BLOCK) // CMP_STRIDE + 1
    idx = jnp.arange(n_cmp)[:, None] * CMP_STRIDE + jnp.arange(CMP_BLOCK)[None, :]
    k_cmp = compress_blocks(k_c[:, :, idx] + cmp_pe_k, cmp_k_w1, cmp_k_w2)
    v_cmp = compress_blocks(v_c[:, :, idx] + cmp_pe_v, cmp_v_w1, cmp_v_w2)
    return (k_cmp, v_cmp, rope_partial(k_s, pos), v_s, rope_partial(k_w, pos), v_w)


def nsa_mixer(h, kv, w_in, w_o):
    k_cmp, v_cmp, k_slc, v_slc, k_win, v_win = kv
    B, S, _ = h.shape
    G, R, dh = N_KV_HEADS, GROUP, HEAD_DIM
    pos = jnp.arange(S)
    scale = HEAD_DIM ** -0.5
    proj = h @ w_in
    q = proj[..., :N_HEADS * dh].reshape(B, S, G, R, dh).transpose(0, 2, 3, 1, 4)
    gates = jax.nn.sigmoid(proj[..., N_HEADS * dh:].astype(jnp.float32))
    gates = gates.reshape(B, S, G, R, N_BRANCH).transpose(0, 2, 3, 1, 4)
    q_rot = rope_partial(q, pos)

    n_cmp = k_cmp.shape[2]
    c_start = jnp.arange(n_cmp) * CMP_STRIDE
    mask_c = (c_start + CMP_BLOCK - 1)[None, :] <= pos[:, None]
    s_c = jnp.einsum("bgrtd,bgnd->bgrtn", q, k_cmp) * scale
    p_c = masked_softmax(s_c, mask_c)
    o_cmp = jnp.einsum("bgrtn,bgnd->bgrtd", p_c.astype(v_cmp.dtype), v_cmp)

    n_sel = S // SEL_BLOCK
    s_start = jnp.arange(n_sel) * SEL_BLOCK
    overlap = ((c_start[:, None] < s_start[None, :] + SEL_BLOCK) &
               (c_start[:, None] + CMP_BLOCK > s_start[None, :])).astype(jnp.float32)
    imp = jnp.einsum("bgrtn,nj->bgtj", p_c, overlap)
    blk = jnp.arange(n_sel)[None, :]
    cur = (pos // SEL_BLOCK)[:, None]
    forced = (blk == 0) | (blk == cur) | (blk == cur - 1)
    valid = blk * SEL_BLOCK <= pos[:, None]
    imp = jnp.where(forced, imp + FORCE_BONUS, imp)
    imp = jnp.where(valid, imp, NEG)
    top_n = min(SEL_TOP_N, n_sel)
    _, sel_idx = lax.top_k(imp, top_n)

    C = SEL_QCHUNK
    n_chunk = S // C
    K = top_n * SEL_BLOCK
    q_ch = q_rot.reshape(B, G, R, n_chunk, C, dh).transpose(3, 0, 1, 2, 4, 5)
    idx_ch = sel_idx.reshape(B, G, n_chunk, C, top_n).transpose(2, 0, 1, 3, 4)
    t_ch = pos.reshape(n_chunk, C)
    b_ix = jnp.arange(B)[:, None, None]
    g_ix = jnp.arange(G)[None, :, None]
    tok_off = jnp.arange(SEL_BLOCK)

    def sel_chunk(args):
        qc, ic, tc = args
        tok = (ic[..., None] * SEL_BLOCK + tok_off).reshape(B, G, C * K)
        ks = k_slc[b_ix, g_ix, tok].reshape(B, G, C, K, dh)
        vs = v_slc[b_ix, g_ix, tok].reshape(B, G, C, K, dh)
        mask = (tok.reshape(B, G, C, K) <= tc[:, None])[:, :, None]
        s = jnp.einsum("bgrcd,bgckd->bgrck", qc, ks) * scale
        p = masked_softmax(s, mask)
        return jnp.einsum("bgrck,bgckd->bgrcd", p.astype(vs.dtype), vs)

    o_slc = lax.map(sel_chunk, (q_ch, idx_ch, t_ch))
    o_slc = o_slc.transpose(1, 2, 3, 0, 4, 5).reshape(B, G, R, S, dh)

    QB = WIN_QBLOCK
    nqb = S // QB
    span = WINDOW + QB
    k_pad = jnp.pad(k_win, ((0, 0), (0, 0), (WINDOW, 0), (0, 0)))
    v_pad = jnp.pad(v_win, ((0, 0), (0, 0), (WINDOW, 0), (0, 0)))
    q_wb = q_rot.reshape(B, G, R, nqb, QB, dh).transpose(3, 0, 1, 2, 4, 5)

    def win_block(args):
        qb, i = args
        start = i * QB
        kb = lax.dynamic_slice_in_dim(k_pad, start, span, axis=2)
        vb = lax.dynamic_slice_in_dim(v_pad, start, span, axis=2)
        tq = start + jnp.arange(QB)
        tk = start - WINDOW + jnp.arange(span)
        diff = tq[:, None] - tk[None, :]
        mask = (diff >= 0) & (diff < WINDOW) & (tk[None, :] >= 0)
        s = jnp.einsum("bgrqd,bgkd->bgrqk", qb, kb) * scale
        p = masked_softmax(s, mask)
        return jnp.einsum("bgrqk,bgkd->bgrqd", p.astype(vb.dtype), vb)

    o_win = lax.map(win_block, (q_wb, jnp.arange(nqb)))
    o_win = o_win.transpose(1, 2, 3, 0, 4, 5).reshape(B, G, R, S, dh)

    o = (gates[..., 0:1] * o_cmp + gates[..., 1:2] * o_slc + gates[..., 2:3] * o_win).astype(h.dtype)
    o = o.transpose(0, 3, 1, 2, 4).reshape(B, S, N_HEADS * dh)
    return o @ w_o


def setup_inputs(seed: int = 0) -> dict:
    key = jax.random.key(seed)
    ks = jax.random.split(key, 32)
    nA, nB = N_CONV_LAYERS, N_ATTN_LAYERS
    D = D_MODEL

    def nrm(k, shape, scale):
        return jax.random.normal(k, shape, jnp.float32) * scale

    def gain(k, shape):
        return 1.0 + 0.05 * jax.random.normal(k, shape, jnp.float32)

    kv_width = N_KV_TENSORS * N_KV_HEADS * HEAD_DIM
    in_width = N_HEADS * HEAD_DIM + N_BRANCH * N_HEADS
    flat = CMP_BLOCK * HEAD_DIM
    return {
        "x": nrm(ks[0], (BATCH, SEQ, D), 1.0),
        "ln_mix_pre": gain(ks[1], (DEPTH, D)),
        "ln_mix_post": gain(ks[2], (DEPTH, D)),
        "ln_mlp_pre": gain(ks[3], (DEPTH, D)),
        "ln_mlp_post": gain(ks[4], (DEPTH, D)),
        "conv_w_pw1": nrm(ks[5], (nA, D, 2 * D), D ** -0.5),
        "conv_b_pw1": nrm(ks[6], (nA, 2 * D), 0.01),
        "conv_w_dw": nrm(ks[7], (nA, CONV_WIDTH, D), CONV_WIDTH ** -0.5),
        "conv_b_dw": nrm(ks[8], (nA, D), 0.01),
        "conv_ln_g": gain(ks[9], (nA, D)),
        "conv_ln_b": nrm(ks[10], (nA, D), 0.01),
        "conv_w_pw2": nrm(ks[11], (nA, D, D), D ** -0.5),
        "conv_b_pw2": nrm(ks[12], (nA, D), 0.01),
        "ln_kv": gain(ks[13], (D,)),
        "w_kv": nrm(ks[14], (D, kv_width), D ** -0.5),
        "cmp_pe_k": nrm(ks[15], (CMP_BLOCK, HEAD_DIM), 0.02),
        "cmp_pe_v": nrm(ks[16], (CMP_BLOCK, HEAD_DIM), 0.02),
        "cmp_k_w1": nrm(ks[17], (flat, CMP_HIDDEN), flat ** -0.5),
        "cmp_k_w2": nrm(ks[18], (CMP_HIDDEN, HEAD_DIM), CMP_HIDDEN ** -0.5),
        "cmp_v_w1": nrm(ks[19], (flat, CMP_HIDDEN), flat ** -0.5),
        "cmp_v_w2": nrm(ks[20], (CMP_HIDDEN, HEAD_DIM), CMP_HIDDEN ** -0.5),
        "attn_w_in": nrm(ks[21], (nB, D, in_width), D ** -0.5),
        "attn_w_o": nrm(ks[22], (nB, N_HEADS * HEAD_DIM, D), (N_HEADS * HEAD_DIM) ** -0.5),
        "mlp_w_up": nrm(ks[23], (DEPTH, D, D_FF), D ** -0.5),
        "mlp_w_down": nrm(ks[24], (DEPTH, D_FF, D), D_FF ** -0.5),
    }


def reference(x, ln_mix_pre, ln_mix_post, ln_mlp_pre, ln_mlp_post,
              conv_w_pw1, conv_b_pw1, conv_w_dw, conv_b_dw, conv_ln_g, conv_ln_b, conv_w_pw2, conv_b_pw2,
              ln_kv, w_kv, cmp_pe_k, cmp_pe_v, cmp_k_w1, cmp_k_w2, cmp_v_w1, cmp_v_w2,
              attn_w_in, attn_w_o, mlp_w_up, mlp_w_down):
    h = x
    kv = None
    for layer in range(DEPTH):
        hn = rmsnorm(h, ln_mix_pre[layer])
        if layer < N_CONV_LAYERS:
            a = conv_module(hn, conv_w_pw1[layer], conv_b_pw1[layer], conv_w_dw[layer], conv_b_dw[layer],
                            conv_ln_g[layer], conv_ln_b[layer], conv_w_pw2[layer], conv_b_pw2[layer])
        else:
            j = layer - N_CONV_LAYERS
            a = nsa_mixer(hn, kv, attn_w_in[j], attn_w_o[j])
        h = h + rmsnorm(a, ln_mix_post[layer])
        m = sq_relu_mlp(rmsnorm(h, ln_mlp_pre[layer]), mlp_w_up[layer], mlp_w_down[layer])
        h = h + rmsnorm(m, ln_mlp_post[layer])
        if layer == N_CONV_LAYERS - 1:
            kv = shared_kv(rmsnorm(h, ln_kv), w_kv, cmp_pe_k, cmp_pe_v,
                           cmp_k_w1, cmp_k_w2, cmp_v_w1, cmp_v_w2)
    return h
```

```python
from contextlib import ExitStack
import numpy as np
import concourse.bass as bass
import concourse.mybir as mybir
from concourse.bass_utils import run_bass_kernel_spmd

F32 = mybir.dt.float32
BF16 = mybir.dt.bfloat16
AF = mybir.ActivationFunctionType
ALU = mybir.AluOpType

D = 2048
KC = 16
S = 2048
T = 512
NT = S // T
DFF = 8192
FC = DFF // 128
CW = 31
HALO = CW - 1
EPS = 1e-6
NH = 16
NG = 4
NCMP = 127
SCALE = 128 ** -0.5
NEGB = -30000.0
SLABN = 256
NPE = 18
EPOCH = 30000
COMPUTE = ("pe", "act", "dve", "pool")
SQ2048 = float(np.sqrt(2048.0))


class Buf:
    __slots__ = ("name", "t", "w", "r", "sem", "dma_cnt", "id")
    _n = 0

    def __init__(self, name, t):
        self.name = name
        self.t = t
        self.w = {}
        self.r = {}
        self.sem = None
        self.dma_cnt = 0
        Buf._n += 1
        self.id = Buf._n

    def __getitem__(self, idx):
        return self.t[idx]


class Instr:
    __slots__ = ("fn", "deps", "me", "waits", "signal", "semval")

    def __init__(self, fn, deps, me):
        self.fn = fn
        self.deps = deps
        self.me = me
        self.waits = None
        self.signal = False
        self.semval = None


class Prog:
    def __init__(self, nc, es):
        self.nc = nc
        self.es = es
        self.streams = {e: [] for e in ("pe", "act", "dve", "pool", "sp")}
        self.ncomp = {e: 0 for e in COMPUTE}
        self.comp_list = {e: [] for e in COMPUTE}
        self.sb_off = 16512
        self.dma_bufs = {}

    def sbuf(self, name, shape, dtype, off=None):
        esz = 4 if dtype == F32 else 2
        per = int(np.prod(shape[1:])) * esz
        if off is None:
            off = self.sb_off
            self.sb_off += (per + 63) // 64 * 64
        t = self.nc.alloc_sbuf_tensor_at(name, list(shape), dtype, offset=off)
        return Buf(name, t)

    def psum(self, name, shape=(128, 512), dtype=F32):
        return Buf(name, self.nc.alloc_psum_tensor(name, list(shape), dtype))

    def dram(self, name, shape, dtype, kind="Internal"):
        return Buf(name, self.nc.dram_tensor(name, list(shape), dtype, kind=kind))

    @staticmethod
    def _deps(reads, writes):
        deps = {}
        for b in reads:
            for k, v in b.w.items():
                if deps.get(k, 0) < v:
                    deps[k] = v
        for b in writes:
            for k, v in b.w.items():
                if deps.get(k, 0) < v:
                    deps[k] = v
            for k, v in b.r.items():
                if deps.get(k, 0) < v:
                    deps[k] = v
        return deps

    @staticmethod
    def _mark(me_key, me_val, reads, writes):
        for b in writes:
            b.w = {me_key: me_val}
            b.r = {}
        for b in reads:
            if b.r.get(me_key, 0) < me_val:
                b.r[me_key] = me_val

    def op(self, eng, fn, reads=(), writes=()):
        deps = self._deps(reads, writes)
        self.ncomp[eng] += 1
        idx = self.ncomp[eng]
        ins = Instr(fn, deps, ("c", eng, idx))
        self.streams[eng].append(ins)
        self.comp_list[eng].append(ins)
        wset = set(id(b) for b in writes)
        self._mark(("c", eng), idx, [b for b in reads if id(b) not in wset], writes)

    def dma(self, q, fn, owner, reads=(), writes=(), extra=None):
        deps = self._deps(reads, writes)
        if extra:
            for k, v in extra.items():
                if deps.get(k, 0) < v:
                    deps[k] = v
        owner.dma_cnt += 1
        self.dma_bufs[owner.id] = owner
        ins = Instr(fn, deps, ("d", owner, owner.dma_cnt))
        self.streams[q].append(ins)
        key = ("d", owner.id)
        for b in writes:
            if b.t.__class__.__name__.startswith("DRam"):
                if b.w.get(key, 0) < owner.dma_cnt:
                    b.w[key] = owner.dma_cnt
                b.r = {}
            else:
                b.w = {key: owner.dma_cnt}
                b.r = {}
        for b in reads:
            if b.r.get(key, 0) < owner.dma_cnt:
                b.r[key] = owner.dma_cnt

    def fence(self, src, dst):
        acc = {}
        for b in src:
            for dct in (b.w, b.r):
                for k, v in dct.items():
                    if acc.get(k, 0) < v:
                        acc[k] = v
        for b in dst:
            for k, v in acc.items():
                if b.r.get(k, 0) < v:
                    b.r[k] = v

    def finish(self):
        nc = self.nc
        for sname, lst in self.streams.items():
            waited = {}
            for ins in lst:
                ws = []
                for k, v in ins.deps.items():
                    if k[0] == "c" and k[1] == "pe" and sname == "pe":
                        continue
                    if waited.get(k, 0) >= v:
                        continue
                    waited[k] = v
                    ws.append((k, v))
                    if k[0] == "c":
                        self.comp_list[k[1]][v - 1].signal = True
                ins.waits = ws
        self.eng_sems = {e: [] for e in COMPUTE}
        for e in COMPUTE:
            cnt = 0
            for ins in self.comp_list[e]:
                if ins.signal:
                    ep, val = divmod(cnt, EPOCH)
                    cnt += 1
                    ins.semval = (ep, val + 1)
            nep = (cnt + EPOCH - 1) // EPOCH if cnt else 0
            for i in range(max(nep, 1)):
                self.eng_sems[e].append(self.es.enter_context(nc.semaphore(f"s_{e}_{i}")))
        for b in self.dma_bufs.values():
            b.sem = self.es.enter_context(nc.semaphore(f"d_{b.name}_{b.id}"))
        nsig = {e: sum(1 for i in self.comp_list[e] if i.signal) for e in COMPUTE}
        print("instr counts", {k: len(v) for k, v in self.streams.items()}, "signals", nsig,
              "dma sems", len(self.dma_bufs), flush=True)

        def replay(sname):
            def run(eng):
                for ins in self.streams[sname]:
                    for k, v in ins.waits:
                        if k[0] == "c":
                            ep, val = self.comp_list[k[1]][v - 1].semval
                            eng.wait_ge(self.eng_sems[k[1]][ep], val)
                        else:
                            eng.wait_ge(self.dma_bufs[k[1]].sem, 16 * v)
                    bi = ins.fn(eng)
                    if ins.me[0] == "d":
                        bi.then_inc(ins.me[1].sem, 16)
                    elif ins.signal:
                        bi.then_inc(self.eng_sems[ins.me[1]][ins.semval[0]], 1)
                if sname == "sp":
                    for b in self.dma_bufs.values():
                        if b.dma_cnt:
                            eng.wait_ge(b.sem, 16 * b.dma_cnt)
            return run

        with nc.Block() as block:
            block.sync(replay("sp"))
            block.tensor(replay("pe"))
            block.scalar(replay("act"))
            block.vector(replay("dve"))
            block.gpsimd(replay("pool"))


class Builder:
    def __init__(self, nseq=2, n_layers=4, dbg=False):
        self.nseq = nseq
        self.n_layers = n_layers
        self.dbg = dbg
        self.attn = n_layers > 2
        self.ntok = nseq * S
        self.nc = bass.Bass("TRN2", target_bir_lowering=False)
        self.es = ExitStack()
        self.P = Prog(self.nc, self.es)
        self.cast_rr = 0
        self.rope_rr = 0
        self.pending_convs = []
        self.conv_seq = []

    def declare(self):
        P = self.P
        n = self.ntok
        self.xT = P.dram("xT", [D, n], F32, kind="ExternalInput")
        self.outT = P.dram("outT", [D, n], F32, kind="ExternalOutput")
        ext = {}

        def inp(name, shape, dtype=F32):
            ext[name] = P.dram(name, shape, dtype, kind="ExternalInput")
            return ext[name]
        self.ext = ext
        inp("vecs", [128, self.NVEC], F32)
        inp("conv_w_dw", [2, 128, KC, CW], F32)
        inp("conv_w_pw1", [2, D, 2 * D])
        inp("conv_w_pw2", [2, D, D])
        inp("mlp_w_up", [4, D, DFF])
        inp("mlp_w_down", [4, DFF, D])
        inp("consts_bf", [128, self.NCONST_BF], BF16)
        if self.attn:
            inp("w_kv", [D, 3072])
            inp("attn_w_in", [2, D, 2096])
            inp("attn_w_o", [2, D, D])
            inp("cmp_k_w1", [4096, 512])
            inp("cmp_v_w1", [4096, 512])
            inp("cmp_k_w2", [512, 128])
            inp("cmp_v_w2", [512, 128])
            inp("cmp_peT", [128, 64])
            inp("consts_f32", [128, self.NCONST_F32], F32)
            inp("ropeC", [128, S])
            inp("ropeS", [128, S])
            inp("topk_bias", [S, 32])
            inp("cmpmask", [128, S], BF16)
            knd = "ExternalOutput" if self.dbg else "Internal"
            self.kc_scr = [P.dram(f"kc_scr{b}", [8, 128, S], BF16, kind=knd) for b in range(self.nseq)]
            self.kT_scr = [P.dram(f"kT_scr{b}", [8, 128, S], BF16, kind=knd) for b in range(self.nseq)]
            self.v_scr = [P.dram(f"v_scr{b}", [8, S, 128], BF16, kind=knd) for b in range(self.nseq)]
            if self.dbg:
                self.dbg_kcmp = P.dram("dbg_kcmp", [128, 8, 128], BF16, kind="ExternalOutput")
                self.dbg_vcmp = P.dram("dbg_vcmp", [128, 8, 129], BF16, kind="ExternalOutput")
        self.hT = [P.dram(f"hT{i}", [D, T], F32) for i in range(self.nseq * NT)]

    VEC_NAMES = ["ln_mix_pre", "ln_mix_post", "ln_mlp_pre", "ln_mlp_post"]
    NVEC = 4 * 4 * KC + 2 * (2 * KC + 5 * KC) + KC
    IDB, CB, WB, EXN = 0, 128, 128 + 2048, 128 + 4096
    NCONST_BF = 128 + 4096 + 2048
    IDF, OVL, ROT, ONEF = 0, 128, 160, 288
    NCONST_F32 = 416

    def make_slabs(self, name, wbuf, lead, kdim, ncols, now=False, after=None):
        P = self.P
        nkg = kdim // 2048
        nng = (ncols + SLABN - 1) // SLABN
        scratch = P.dram(f"ws_{name}", [nkg * nng, 128, KC * SLABN], BF16)
        slabs = {}
        for kg in range(nkg):
            for ng in range(nng):
                w = min(SLABN, ncols - ng * SLABN)
                if lead is None:
                    src = wbuf.t[kg * 2048:(kg + 1) * 2048, ng * SLABN:ng * SLABN + w]
                else:
                    src = wbuf.t[lead, kg * 2048:(kg + 1) * 2048, ng * SLABN:ng * SLABN + w]
                src = src.rearrange("(kc p) n -> p kc n", p=128)
                sidx = kg * nng + ng
                dst = scratch.t[sidx].rearrange("p (kc n) -> p kc n", n=SLABN)[:, :, 0:w]

                def conv(src=src, dst=dst, scratch=scratch, wbuf=wbuf):
                    extra = None
                    if now and len(self.conv_seq) >= 2:
                        ob, oc = self.conv_seq[-2]
                        extra = {("d", ob.id): oc}
                    P.dma("pool", lambda e: e.dma_start(out=dst, in_=src), scratch, reads=[wbuf], writes=[scratch], extra=extra)
                    if now:
                        self.conv_seq.append((scratch, scratch.dma_cnt))
                if now:
                    conv()
                else:
                    self.pending_convs.append(conv)
                slabs[(kg, ng)] = (scratch, sidx, w)
        return slabs

    def flush_convs(self, n, gate=True):
        if gate and n > 0 and self.pending_convs:
            self.P.op("pool", lambda e: e.memset(self.pace.t[:, :], 0.0), reads=[self.xn[0]], writes=[self.pace])
        for _ in range(min(n, len(self.pending_convs))):
            self.pending_convs.pop(0)()

    def load_slab(self, slab):
        P = self.P
        scratch, sidx, w = slab
        sl = self.slots[self.slot_rr % len(self.slots)]
        self.slot_rr += 1
        src = scratch.t[sidx].rearrange("p (kc n) -> p kc n", n=SLABN)[:, :, 0:w]
        P.dma("sp", lambda e, sl=sl, src=src, w=w: e.dma_start(out=sl.t[:, :, 0:w], in_=src),
              sl, reads=[scratch], writes=[sl])
        return sl

    def mm(self, out_buf, out_ap, lhsT_ap, rhs_ap, start, stop, reads, sgc=False):
        if sgc:
            self.P.op("pe", lambda e: e.matmul(out_ap, lhsT_ap, rhs_ap, start=start, stop=stop, skip_group_check=True),
                      reads=reads, writes=[out_buf])
        else:
            self.P.op("pe", lambda e: e.matmul(out_ap, lhsT_ap, rhs_ap, start=start, stop=stop),
                      reads=reads, writes=[out_buf])

    def next_bank(self):
        b = self.banks[self.bank_rr % len(self.banks)]
        self.bank_rr += 1
        return b

    def proj_fm(self, x_chunks, slab_list, nk, consume):
        pend = []
        ci = 0
        for kgs in slab_list:
            w = kgs[0][2]
            nch = (w + 127) // 128
            banks = [self.next_bank() for _ in range(nch)]
            for kgi, slab in enumerate(kgs):
                sl = self.load_slab(slab)
                for j in range(nch):
                    m = min(128, w - j * 128)
                    for kc in range(KC):
                        xk = x_chunks[kgi * KC + kc]
                        self.mm(banks[j], banks[j].t[0:m, :], sl.t[:, kc, j * 128:j * 128 + m], xk.t[:, :],
                                start=(kgi == 0 and kc == 0), stop=(kgi == len(kgs) - 1 and kc == KC - 1),
                                reads=[sl, xk])
            for f in pend:
                f()
            pend = []
            for j in range(nch):
                f = consume(ci, banks[j])
                if f is not None:
                    pend.append(f)
                ci += 1
        for f in pend:
            f()

    def stats_add(self, bank, src_buf, first, last):
        self.mm(bank, bank.t[:, :], self.ones_bf.t[:, 0:128], src_buf.t[:, :], first, last, [self.ones_bf, src_buf])

    def sq_to(self, dst, src_buf, src_ap):
        self.P.op("act", lambda e: e.activation(out=dst.t[:, :], in_=src_ap, func=AF.Square),
                  reads=[src_buf], writes=[dst])

    def rstd_from(self, bank, dst, scale=1.0 / D):
        self.pow_act(bank, dst, scale, 1, -0.5)

    def pow_act(self, src, dst, scale, bias_col, power, parts=128):
        self.P.op("act", lambda e: e.activation(out=dst.t[0:parts, :], in_=src.t[0:parts, :], func=AF.Ln,
                                                bias=self.epsb.t[0:parts, bias_col:bias_col + 1], scale=scale),
                  reads=[src, self.epsb], writes=[dst])
        self.P.op("act", lambda e: e.activation(out=dst.t[0:parts, :], in_=dst.t[0:parts, :], func=AF.Exp, scale=power),
                  reads=[dst], writes=[dst])

    def vec(self, col):
        return self.vecs.t[:, col:col + 1]

    def rmsnorm_to_bf(self, src_chunks, gcol, dst_chunks):
        P = self.P
        bank = self.stat_banks[0]
        for kc in range(KC):
            sq = self.sqb[kc % 2]
            self.sq_to(sq, src_chunks[kc], src_chunks[kc].t[:, :])
            self.stats_add(bank, sq, kc == 0, kc == KC - 1)
        self.rstd_from(bank, self.rstd)
        for kc in range(KC):
            P.op("dve", lambda e, kc=kc: e.scalar_tensor_tensor(
                out=dst_chunks[kc].t[:, :], in0=src_chunks[kc].t[:, :], scalar=self.vec(gcol + kc),
                in1=self.rstd.t[:, :], op0=ALU.mult, op1=ALU.mult),
                reads=[src_chunks[kc], self.rstd, self.vecs], writes=[dst_chunks[kc]])

    def resid_add_norm(self, gcol):
        P = self.P
        self.rstd_from(self.stat_banks[1], self.rstd2)
        for kc in range(KC):
            u = self.ub[kc % 2]
            P.op("dve", lambda e, kc=kc, u=u: e.scalar_tensor_tensor(
                out=u.t[:, :], in0=self.tmp32[kc].t[:, :], scalar=self.vec(gcol + kc),
                in1=self.rstd2.t[:, :], op0=ALU.mult, op1=ALU.mult),
                reads=[self.tmp32[kc], self.rstd2, self.vecs], writes=[u])
            P.op("dve", lambda e, kc=kc, u=u: e.tensor_tensor(
                out=self.h[kc].t[:, :], in0=self.h[kc].t[:, :], in1=u.t[:, :], op=ALU.add),
                reads=[self.h[kc], u], writes=[self.h[kc]])

    def evac_with_stats(self, bank, kc, bias_col, first, last):
        P = self.P
        dst = self.tmp32[kc]
        if bias_col is None:
            P.op("act", lambda e: e.activation(out=dst.t[:, :], in_=bank.t[:, :], func=AF.Copy),
                 reads=[bank], writes=[dst])
        else:
            P.op("act", lambda e: e.activation(out=dst.t[:, :], in_=bank.t[:, :], func=AF.Identity,
                                               bias=self.vec(bias_col)),
                 reads=[bank, self.vecs], writes=[dst])
        sq = self.sqb[kc % 2]
        self.sq_to(sq, dst, dst.t[:, :])
        return lambda: self.stats_add(self.stat_banks[1], sq, first, last)

    def mlp(self, layer):
        P = self.P
        V = self.vcol
        self.rmsnorm_to_bf(self.h, V["ln_mlp_pre"] + layer * KC, self.xn)

        def relu2(ci, bank):
            r = self.relub[ci % 2]
            P.op("act", lambda e: e.activation(out=r.t[:, :], in_=bank.t[:, :], func=AF.Relu),
                 reads=[bank], writes=[r])
            P.op("act", lambda e: e.activation(out=self.hidden[ci].t[:, :], in_=r.t[:, :], func=AF.Square),
                 reads=[r], writes=[self.hidden[ci]])
        up = self.slabs[f"up{layer}"]
        self.proj_fm(self.xn, [[up[(0, ng)]] for ng in range(DFF // SLABN)], 1, relu2)
        dn = self.slabs[f"down{layer}"]

        def evac(ci, bank):
            return self.evac_with_stats(bank, ci, None, ci == 0, ci == KC - 1)
        self.proj_fm(self.hidden, [[dn[(kg, ng)] for kg in range(4)] for ng in range(D // SLABN)], 4, evac)
        self.resid_add_norm(V["ln_mlp_post"] + layer * KC)

    def conv_mixer(self, layer, first_tile):
        P = self.P
        V = self.vcol
        self.rmsnorm_to_bf(self.h, V["ln_mix_pre"] + layer * KC, self.xn)
        pw1 = self.slabs[f"pw1_{layer}"]
        order = []
        for j in range(8):
            order.append([pw1[(0, 8 + j)]])
            order.append([pw1[(0, j)]])
        bcol = V["conv_b_pw1"] + layer * 2 * KC
        wdw = self.wdw[layer]

        pair = {}

        def consume(ci, bank):
            grp, j = divmod(ci, 4)
            if j < 2:
                c = 2 * grp + j
                sg = self.sig[j]
                P.op("act", lambda e: e.activation(out=sg.t[:, :], in_=bank.t[:, :], func=AF.Sigmoid,
                                                   bias=self.vec(bcol + KC + c)),
                     reads=[bank, self.vecs], writes=[sg])
                return
            c = 2 * grp + (j - 2)
            sg = self.sig[j - 2]
            gl = self.glu[c % 2]
            glb = gl
            dg = self.diag[c % 2]
            if first_tile:
                P.op("dve", lambda e: e.memset(gl.t[:, 0:HALO], 0.0), reads=[], writes=[gl])
            else:
                P.op("dve", lambda e: e.tensor_copy(out=gl.t[:, 0:HALO], in_=self.halo.t[:, c, :]),
                     reads=[self.halo_b[c]], writes=[gl])
            P.op("dve", lambda e: e.scalar_tensor_tensor(out=gl.t[:, HALO:HALO + T], in0=bank.t[:, :],
                                                         scalar=self.vec(bcol + c), in1=sg.t[:, :],
                                                         op0=ALU.add, op1=ALU.mult),
                 reads=[bank, sg, self.vecs], writes=[gl])
            P.op("dve", lambda e: e.tensor_copy(out=self.halo.t[:, c, :], in_=gl.t[:, T:T + HALO]),
                 reads=[gl], writes=[self.halo_b[c]])
            for k in range(NPE):
                dbuf, di = dg[k]
                P.op("act", lambda e, k=k, dbuf=dbuf, di=di: e.activation(out=dbuf.t[:, di, :], in_=self.identc.t[:, :], func=AF.Copy,
                                                                          scale=wdw.t[:, c, k:k + 1]),
                     reads=[self.identc, wdw], writes=[dbuf])
            pair[j - 2] = (c, gl, glb, dg)
            if j < 3:
                return
            prs = [pair[0], pair[1]]
            for (cc_, gl_, glb_, dg_) in prs:
                P.op("dve", lambda e, cc_=cc_, gl_=gl_: e.tensor_scalar(
                    out=self.tmp32[cc_].t[:, :], in0=gl_.t[:, NPE:NPE + T], scalar1=wdw.t[:, cc_, NPE:NPE + 1],
                    scalar2=self.vec(V["conv_b_dw"] + layer * KC + cc_), op0=ALU.mult, op1=ALU.add),
                    reads=[gl_, wdw, self.vecs], writes=[self.tmp32[cc_]])
            for k in range(NPE + 1, CW):
                for (cc_, gl_, glb_, dg_) in prs:
                    P.op("dve", lambda e, k=k, cc_=cc_, gl_=gl_: e.scalar_tensor_tensor(
                        out=self.tmp32[cc_].t[:, :], in0=gl_.t[:, k:k + T], scalar=wdw.t[:, cc_, k:k + 1],
                        in1=self.tmp32[cc_].t[:, :], op0=ALU.mult, op1=ALU.add),
                        reads=[gl_, wdw, self.tmp32[cc_]], writes=[self.tmp32[cc_]])

            def later():
                bks = [self.next_bank(), self.next_bank()]
                for pi, (cc_, gl_, glb_, dg_) in enumerate(prs):
                    bk = bks[pi]
                    for k in range(NPE):
                        self.mm(bk, bk.t[:, :], dg_[k][0].t[:, dg_[k][1], :], glb_.t[:, k:k + T], k == 0, k == NPE - 1, [dg_[k][0], glb_])
                for pi, (cc_, gl_, glb_, dg_) in enumerate(prs):
                    bk = bks[pi]
                    P.op("dve", lambda e, cc_=cc_, bk=bk: e.tensor_tensor(
                        out=self.tmp32[cc_].t[:, :], in0=self.tmp32[cc_].t[:, :], in1=bk.t[:, :], op=ALU.add),
                        reads=[self.tmp32[cc_], bk], writes=[self.tmp32[cc_]])
                for pi, (cc_, gl_, glb_, dg_) in enumerate(prs):
                    y = self.tmp32[cc_]
                    sq = self.sqb[cc_ % 2]
                    self.sq_to(sq, y, y.t[:, :])
                    self.stats_add(self.stat_banks[1], sq, cc_ == 0, cc_ == KC - 1)
                    yb = self.ybf[cc_ % 2]
                    P.op("act", lambda e, y=y, yb=yb: e.activation(out=yb.t[:, :], in_=y.t[:, :], func=AF.Copy), reads=[y], writes=[yb])
                    self.stats_add(self.stat_banks[0], yb, cc_ == 0, cc_ == KC - 1)
            return later
        self.proj_fm(self.xn, order, 1, consume)
        mean = self.rstd
        rs = self.rstd2
        msq = self.ub[0]
        P.op("dve", lambda e: e.tensor_scalar(out=mean.t[:, :], in0=self.stat_banks[0].t[:, :], scalar1=1.0 / D,
                                              scalar2=None, op0=ALU.mult),
             reads=[self.stat_banks[0]], writes=[mean])
        P.op("dve", lambda e: e.tensor_tensor(out=msq.t[:, :], in0=mean.t[:, :], in1=mean.t[:, :], op=ALU.mult),
             reads=[mean], writes=[msq])
        P.op("dve", lambda e: e.scalar_tensor_tensor(out=msq.t[:, :], in0=self.stat_banks[1].t[:, :], scalar=1.0 / D,
                                                     in1=msq.t[:, :], op0=ALU.mult, op1=ALU.subtract),
             reads=[self.stat_banks[1], msq], writes=[msq])
        self.rstd_from(msq, rs, scale=1.0)
        for c in range(KC):
            y = self.tmp32[c]
            u = self.ub[c % 2]
            P.op("dve", lambda e, y=y, u=u: e.tensor_tensor(out=u.t[:, :], in0=y.t[:, :], in1=mean.t[:, :], op=ALU.subtract),
                 reads=[y, mean], writes=[u])
            P.op("dve", lambda e, u=u: e.tensor_tensor(out=u.t[:, :], in0=u.t[:, :], in1=rs.t[:, :], op=ALU.mult),
                 reads=[u, rs], writes=[u])
            P.op("act", lambda e, u=u, c=c: e.activation(out=self.xn[c].t[:, :], in_=u.t[:, :], func=AF.Silu,
                                                         scale=self.vec(V["conv_ln_g"] + layer * KC + c),
                                                         bias=self.vec(V["conv_ln_b"] + layer * KC + c)),
                 reads=[u, self.vecs], writes=[self.xn[c]])
        pw2 = self.slabs[f"pw2_{layer}"]

        def evac(ci, bank):
            return self.evac_with_stats(bank, ci, V["conv_b_pw2"] + layer * KC + ci, ci == 0, ci == KC - 1)
        self.proj_fm(self.xn, [[pw2[(0, ng)]] for ng in range(D // SLABN)], 1, evac)
        self.resid_add_norm(V["ln_mix_post"] + layer * KC)


    def rope(self, bank, dst16, plain16=None):
        P = self.P
        xs = self.relub[self.rope_rr % 2]
        self.rope_rr += 1
        ropeC, ropeS = self.tmp32[4], self.tmp32[5]
        bankR = self.stat_banks[1]
        P.op("act", lambda e: e.activation(out=xs.t[:, :], in_=bank.t[:, :], func=AF.Copy), reads=[bank], writes=[xs])
        if plain16 is not None:
            P.op("act", lambda e: e.activation(out=plain16.t[:, :], in_=bank.t[:, :], func=AF.Copy), reads=[bank], writes=[plain16])

        def later():
            rot = self.cf32.t[:, self.ROT:self.ROT + 128]
            self.mm(bankR, bankR.t[:, :], rot, xs.t[:, :], True, True, [self.cf32, xs])
            t1, t2 = self.ub[0], self.ub[1]
            P.op("dve", lambda e: e.tensor_tensor(out=t1.t[:, :], in0=bankR.t[:, :], in1=ropeS.t[:, :], op=ALU.mult),
                 reads=[bankR, ropeS], writes=[t1])
            P.op("dve", lambda e: e.tensor_tensor(out=t2.t[:, :], in0=xs.t[:, :], in1=ropeC.t[:, :], op=ALU.mult),
                 reads=[xs, ropeC], writes=[t2])
            P.op("dve", lambda e: e.tensor_tensor(out=dst16.t[:, :], in0=t1.t[:, :], in1=t2.t[:, :], op=ALU.add),
                 reads=[t1, t2], writes=[dst16])
        return later

    def load_rope(self, ti):
        P = self.P
        for nm, dst in (("ropeC", self.tmp32[4]), ("ropeS", self.tmp32[5])):
            src = self.ext[nm]
            P.dma("sp", lambda e, dst=dst, src=src: e.dma_start(out=dst.t[:, :], in_=src.t[:, ti * T:(ti + 1) * T]),
                  dst, reads=[src], writes=[dst])

    def kv_phase(self, b, ti):
        P = self.P
        V = self.vcol
        P.fence(self.hidden, self.kv_stage)
        self.rmsnorm_to_bf(self.h, V["ln_kv"], self.xn)
        if getattr(self, "kv_next", None) is not None:
            self.load_h(*self.kv_next)
        self.load_rope(ti)
        wkv = self.slabs["wkv"]
        cnt = [0]

        def fm_consume(tensor_i):
            def consume(ci, bank):
                g = ci
                st = self.kst[cnt[0] % 2]
                cnt[0] += 1
                if tensor_i in (0, 1):
                    P.op("act", lambda e: e.activation(out=st.t[:, :], in_=bank.t[:, :], func=AF.Copy),
                         reads=[bank], writes=[st])
                    scr = self.kc_scr[b]
                    dst = scr.t[tensor_i * 4 + g][:, ti * T:(ti + 1) * T]
                    P.dma("sp", lambda e: e.dma_start(out=dst, in_=st.t[:, :]), st, reads=[st], writes=[scr])
                    return None
                lat = self.rope(bank, st)
                scr = self.kT_scr[b]
                which = 0 if tensor_i == 2 else 1
                dst = scr.t[which * 4 + g][:, ti * T:(ti + 1) * T]

                def later():
                    lat()
                    P.dma("sp", lambda e: e.dma_start(out=dst, in_=st.t[:, :]), st, reads=[st], writes=[scr])
                return later
            return consume
        for tensor_i in (0, 1, 2, 4):
            self.proj_fm(self.xn, [[wkv[(0, 2 * tensor_i)]], [wkv[(0, 2 * tensor_i + 1)]]], 1, fm_consume(tensor_i))
        for which, tensor_i in ((0, 3), (1, 5)):
            for half in range(2):
                sl = self.load_slab(wkv[(0, 2 * tensor_i + half)])
                st = self.vst[cnt[0] % 2]
                cnt[0] += 1
                for tb in range(4):
                    bank = self.next_bank()
                    for kc in range(KC):
                        self.mm(bank, bank.t[:, 0:256], self.xn[kc].t[:, tb * 128:(tb + 1) * 128], sl.t[:, kc, :],
                                kc == 0, kc == KC - 1, [sl, self.xn[kc]])
                    P.op("act", lambda e, bank=bank, tb=tb, st=st: e.activation(out=st.t[:, tb, :], in_=bank.t[:, 0:256], func=AF.Copy),
                         reads=[bank], writes=[st])
                scr = self.v_scr[b]
                for gg in range(2):
                    g = half * 2 + gg
                    dst = scr.t[which * 4 + g][ti * T:(ti + 1) * T, :].rearrange("(tb p) d -> p tb d", p=128)
                    P.dma("sp", lambda e, dst=dst, st=st, gg=gg: e.dma_start(out=dst, in_=st.t[:, :, gg * 128:(gg + 1) * 128]),
                          st, reads=[st], writes=[scr])
        P.fence(self.kv_stage, self.hidden)

    def compress_phase(self):
        P = self.P
        E = self.ext
        P.fence(self.hidden + self.kv_stage, self.cmp_bufs)
        P.dma("sp", lambda e: e.dma_start(out=self.peT32.t[:, :], in_=E["cmp_peT"].t[:, :]), self.peT32,
              reads=[E["cmp_peT"]], writes=[self.peT32])
        P.op("dve", lambda e: e.tensor_copy(out=self.peT16.t[:, :], in_=self.peT32.t[:, :]), reads=[self.peT32], writes=[self.peT16])
        for kv in range(2):
            self._compress_one(kv)
        P.fence(self.cmp_bufs, self.hidden + self.attn_bufs)

    def _compress_one(self, kv):
        P = self.P
        E = self.ext
        if True:
            w1 = self.slabs["cmp_k_w1" if kv == 0 else "cmp_v_w1"]
            w2src = E["cmp_k_w2" if kv == 0 else "cmp_v_w2"]
            w2f, w2b = self.w2f[kv], self.w2b[kv]
            P.dma("sp", lambda e, w2f=w2f, w2src=w2src: e.dma_start(out=w2f.t[:, :, :], in_=w2src.t[:, :].rearrange("(hc p) d -> p hc d", p=128)),
                  w2f, reads=[w2src], writes=[w2f])
            P.op("dve", lambda e, w2f=w2f, w2b=w2b: e.tensor_copy(out=w2b.t[:, :, :], in_=w2f.t[:, :, :]), reads=[w2f], writes=[w2b])
            for b in range(self.nseq):
                for g in range(NG):
                    kb = self.kcbuf[b * 4 + g]
                    src = self.kc_scr[b]
                    P.dma("sp", lambda e, kb=kb, src=src, g=g: e.dma_start(out=kb.t[:, :], in_=src.t[kv * 4 + g]),
                          kb, reads=[src], writes=[kb])
            for ng in range(2):
                sls = [self.load_slab(w1[(kg, ng)]) for kg in range(2)]
                bankB = self.next_bank()
                for j in range(2):
                    for l in range(32):
                        self.mm(bankB, bankB.t[:, j:j + 1], sls[l // 16].t[:, l % 16, j * 128:(j + 1) * 128],
                                self.peT16.t[:, kv * 32 + l:kv * 32 + l + 1], l == 0, l == 31, [sls[l // 16], self.peT16])
                P.op("act", lambda e, bankB=bankB, ng=ng: e.activation(out=self.biash.t[:, kv * 4 + ng * 2:kv * 4 + ng * 2 + 2], in_=bankB.t[:, 0:2], func=AF.Copy),
                     reads=[bankB], writes=[self.biash])
                for b in range(self.nseq):
                    for g in range(NG):
                        kb = self.kcbuf[b * 4 + g]
                        hc_buf = self.hidc[b * 4 + g]
                        for j in range(2):
                            bank = self.next_bank()
                            for l in range(32):
                                self.mm(bank, bank.t[:, 0:NCMP], sls[l // 16].t[:, l % 16, j * 128:(j + 1) * 128],
                                        kb.t[:, l:l + 16 * (NCMP - 1) + 1:16], l == 0, l == 31, [sls[l // 16], kb])
                            hc = ng * 2 + j
                            P.op("act", lambda e, bank=bank, hc_buf=hc_buf, hc=hc: e.activation(
                                out=hc_buf.t[:, hc, 0:NCMP], in_=bank.t[:, 0:NCMP], func=AF.Silu,
                                bias=self.biash.t[:, kv * 4 + hc:kv * 4 + hc + 1]),
                                reads=[bank, self.biash], writes=[hc_buf])
            for b in range(self.nseq):
                for g in range(NG):
                    hc_buf = self.hidc[b * 4 + g]
                    bank = self.next_bank()
                    if kv == 0:
                        for hc in range(4):
                            self.mm(bank, bank.t[:, 0:NCMP], w2b.t[:, hc, :], hc_buf.t[:, hc, 0:NCMP], hc == 0, hc == 3, [w2b, hc_buf])
                        dstb = self.kcmpT_b[b * 4 + g]
                        P.op("act", lambda e, bank=bank, dstb=dstb: e.activation(out=dstb.t[:, 0:NCMP], in_=bank.t[:, 0:NCMP], func=AF.Copy),
                             reads=[bank], writes=[dstb])
                    else:
                        for hc in range(4):
                            self.mm(bank, bank.t[0:NCMP, 0:128], hc_buf.t[:, hc, 0:NCMP], w2b.t[:, hc, :], hc == 0, hc == 3, [w2b, hc_buf])
                        dstb = self.vcmp_b[b * 4 + g]
                        P.op("act", lambda e, bank=bank, dstb=dstb: e.activation(out=dstb.t[0:NCMP, 0:128], in_=bank.t[0:NCMP, 0:128], func=AF.Copy),
                             reads=[bank], writes=[dstb])

    def attn_mixer(self, j, b, qt):
        P = self.P
        V = self.vcol
        layer = 2 + j
        P.fence(self.hidden, self.attn_bufs)
        self.rmsnorm_to_bf(self.h, V["ln_mix_pre"] + layer * KC, self.xn)
        self.load_rope(qt)
        E = self.ext
        P.dma("sp", lambda e: e.dma_start(out=self.tkb.t[:, :].rearrange("p (qb j) -> p qb j", j=32),
                                           in_=E["topk_bias"].t[qt * T:(qt + 1) * T, :].rearrange("(qb p) j -> p qb j", p=128)),
              self.tkb, reads=[E["topk_bias"]], writes=[self.tkb])
        P.dma("sp", lambda e: e.dma_start(out=self.cmpmask.t[:, :], in_=E["cmpmask"].t[:, qt * T:(qt + 1) * T]),
              self.cmpmask, reads=[E["cmpmask"]], writes=[self.cmpmask])
        win = self.slabs[f"win{j}"]
        gt = self.tmp32[6]

        def qcons(ci, bank):
            return self.rope(bank, self.qrot[ci], self.q16[ci])
        self.proj_fm(self.xn, [[win[(0, ng)]] for ng in range(8)], 1, qcons)

        bS = self.banks[0:3]
        bOsets = [(self.banks[3], self.banks[4]), (self.banks[5], self.stat_banks[0])]
        bX = self.stat_banks[1]
        slg = self.load_slab(win[(0, 8)])
        for qb in range(4):
            for kc in range(KC):
                self.mm(bX, bX.t[:, qb * 48:(qb + 1) * 48], self.xn[kc].t[:, qb * 128:(qb + 1) * 128], slg.t[:, kc, 0:48],
                        kc == 0, kc == KC - 1, [slg, self.xn[kc]])
        P.op("act", lambda e: e.activation(out=gt.t[:, 0:192], in_=bX.t[:, 0:192], func=AF.Sigmoid), reads=[bX], writes=[gt])

        cb = self.cbf
        identb = cb.t[:, self.IDB:self.IDB + 128]
        identf = self.cf32.t[:, self.IDF:self.IDF + 128]
        rd4b, c4b = self.rstd, self.rstd2
        Ebufs = [self.sqb[0], self.sqb[1], self.eb2]
        nk = (qt + 1) * T
        k0 = max(0, qt * T - T)
        st = dict(s=0, e=0, hd=0, ep=0)

        def zero_init(pair, h):
            for bk in pair:
                self.mm(bk, bk.t[:, 0:258], self.zeros_bf.t[:, :], self.qrot[h].t[:, 0:258], True, True, [self.zeros_bf, self.qrot[h]], sgc=True)

        def pv(pair, Eb, parts, vap, vbuf, qbs):
            for qb in qbs:
                bk = pair[qb // 2]
                off = (qb % 2) * 129
                self.mm(bk, bk.t[:, off:off + 129], Eb.t[0:parts, qb * 128:(qb + 1) * 128], vap, False, True, [vbuf, Eb], sgc=True)

        def epilogue(pair, h, br, r, eps, first, final):
            k = st["ep"] % 64
            st["ep"] += 1
            rd4 = rd4b.t[:, k * 4:(k + 1) * 4]
            c4 = c4b.t[:, k * 4:(k + 1) * 4]
            for hi, bk in enumerate(pair):
                dv = bk.t[:, 128:258:129]
                if eps:
                    P.op("dve", lambda e, dv=dv, hi=hi: e.tensor_scalar(out=rd4[:, 2 * hi:2 * hi + 2], in0=dv, scalar1=1e-30, scalar2=None, op0=ALU.add),
                         reads=[bk], writes=[rd4b])
                    P.op("dve", lambda e, hi=hi: e.reciprocal(out=rd4[:, 2 * hi:2 * hi + 2], in_=rd4[:, 2 * hi:2 * hi + 2]), reads=[rd4b], writes=[rd4b])
                else:
                    P.op("dve", lambda e, dv=dv, hi=hi: e.reciprocal(out=rd4[:, 2 * hi:2 * hi + 2], in_=dv), reads=[bk], writes=[rd4b])
            row = h * 3 + br
            P.op("dve", lambda e: e.tensor_tensor(out=c4, in0=rd4, in1=gt.t[:, row:row + 145:48], op=ALU.mult),
                 reads=[rd4b, gt], writes=[c4b])
            acc = self.tmp32[r]
            for qb in range(4):
                bk = pair[qb // 2]
                off = (qb % 2) * 129
                if first:
                    P.op("dve", lambda e, bk=bk, off=off, qb=qb: e.tensor_scalar(
                        out=acc.t[:, qb * 128:(qb + 1) * 128], in0=bk.t[:, off:off + 128], scalar1=c4[:, qb:qb + 1], scalar2=None, op0=ALU.mult),
                        reads=[bk, c4b], writes=[acc])
                else:
                    P.op("dve", lambda e, bk=bk, off=off, qb=qb: e.scalar_tensor_tensor(
                        out=acc.t[:, qb * 128:(qb + 1) * 128], in0=bk.t[:, off:off + 128], scalar=c4[:, qb:qb + 1],
                        in1=acc.t[:, qb * 128:(qb + 1) * 128], op0=ALU.mult, op1=ALU.add),
                        reads=[bk, c4b, acc], writes=[acc])
            if final:
                for qb in range(4):
                    P.op("pe", lambda e, qb=qb: e.transpose(bX.t[:, qb * 128:(qb + 1) * 128], acc.t[:, qb * 128:(qb + 1) * 128], identf),
                         reads=[acc, self.cf32], writes=[bX])
                P.op("act", lambda e: e.activation(out=self.xn[h].t[:, :], in_=bX.t[:, :], func=AF.Copy), reads=[bX], writes=[self.xn[h]])
            return rd4

        for g in range(NG):
            kvs = self.kvset[g % 2]
            ksT, vs, kwT, vw = kvs["ksT"], kvs["vs"], kvs["kwT"], kvs["vw"]
            kscr, vscr = self.kT_scr[b], self.v_scr[b]
            P.dma("sp", lambda e, ksT=ksT, g=g: e.dma_start(out=ksT.t[:, 0:nk], in_=kscr.t[g][:, 0:nk]),
                  ksT, reads=[kscr], writes=[ksT])
            P.dma("sp", lambda e, vs=vs, g=g: e.dma_start(out=vs.t[:, 0:nk // 128, 0:128],
                                                           in_=vscr.t[g][0:nk, :].rearrange("(kt p) d -> p kt d", p=128)),
                  vs, reads=[vscr], writes=[vs])
            P.op("pool", lambda e, vs=vs: e.memset(vs.t[:, 0:nk // 128, 128:129], 1.0), writes=[vs])
            P.dma("sp", lambda e, kwT=kwT, g=g: e.dma_start(out=kwT.t[:, 0:nk - k0], in_=kscr.t[4 + g][:, k0:nk]),
                  kwT, reads=[kscr], writes=[kwT])
            P.dma("sp", lambda e, vw=vw, g=g: e.dma_start(out=vw.t[:, 0:(nk - k0) // 128, 0:128],
                                                           in_=vscr.t[4 + g][k0:nk, :].rearrange("(kt p) d -> p kt d", p=128)),
                  vw, reads=[vscr], writes=[vw])
            P.op("pool", lambda e, vw=vw: e.memset(vw.t[:, 0:(nk - k0) // 128, 128:129], 1.0), writes=[vw])
            kc_b = self.kcmpT_b[b * 4 + g]
            vc_b = self.vcmp_b[b * 4 + g]
            ovl = self.cf32.t[0:NCMP, self.OVL:self.OVL + 32]

            def cmp_S(r):
                h = 4 * g + r
                bank = bS[st["s"] % 3]
                st["s"] += 1
                self.mm(bank, bank.t[0:NCMP, :], kc_b.t[:, 0:NCMP], self.q16[h].t[:, :], True, False, [kc_b, self.q16[h]])
                self.mm(bank, bank.t[0:NCMP, :], cb.t[0:NCMP, self.IDB:self.IDB + NCMP], self.cmpmask.t[0:NCMP, :], False, True,
                        [cb, self.cmpmask])
                return bank

            def cmp_rest(r, bank):
                h = 4 * g + r
                Eb = Ebufs[st["e"] % 3]
                st["e"] += 1
                Ec32 = self.Ec[r % 2]
                P.op("act", lambda e: e.activation(out=Eb.t[0:NCMP, :], in_=bank.t[0:NCMP, :], func=AF.Exp, scale=SCALE),
                     reads=[bank], writes=[Eb])
                P.op("act", lambda e: e.activation(out=Ec32.t[0:NCMP, :], in_=bank.t[0:NCMP, :], func=AF.Exp, scale=SCALE),
                     reads=[bank], writes=[Ec32])
                pair = bOsets[st["hd"] % 2]
                st["hd"] += 1
                zero_init(pair, h)
                pv(pair, Eb, NCMP, vc_b.t[0:NCMP, 0:129], vc_b, range(4))
                for qb in range(4):
                    c_ = r * 128 + qb * 32
                    self.mm(bX, bX.t[:, c_:c_ + 32], Ec32.t[0:NCMP, qb * 128:(qb + 1) * 128], ovl, True, True, [Ec32, self.cf32], sgc=True)
                rd4 = epilogue(pair, h, 0, r, True, True, False)
                for qb in range(4):
                    c_ = r * 128 + qb * 32
                    if r == 0:
                        P.op("dve", lambda e, qb=qb, c_=c_: e.tensor_scalar(out=self.imp2.t[:, qb * 32:(qb + 1) * 32], in0=bX.t[:, c_:c_ + 32],
                                                                          scalar1=rd4[:, qb:qb + 1], scalar2=None, op0=ALU.mult),
                             reads=[bX, rd4b], writes=[self.imp2])
                    else:
                        P.op("dve", lambda e, qb=qb, c_=c_: e.scalar_tensor_tensor(out=self.imp2.t[:, qb * 32:(qb + 1) * 32], in0=bX.t[:, c_:c_ + 32],
                                                                                 scalar=rd4[:, qb:qb + 1], in1=self.imp2.t[:, qb * 32:(qb + 1) * 32],
                                                                                 op0=ALU.mult, op1=ALU.add),
                             reads=[bX, rd4b, self.imp2], writes=[self.imp2])
            cbanks = {0: cmp_S(0)}
            for r in range(4):
                if r + 1 < 4:
                    cbanks[r + 1] = cmp_S(r + 1)
                cmp_rest(r, cbanks[r])
            P.op("dve", lambda e: e.tensor_tensor(out=self.imp2.t[:, :], in0=self.imp2.t[:, :], in1=self.tkb.t[:, :], op=ALU.add),
                 reads=[self.imp2, self.tkb], writes=[self.imp2])
            for qb in range(4):
                iv = self.imp2.t[:, qb * 32:(qb + 1) * 32]
                m1, m2, wk = self.m1[qb], self.m2[qb], self.tkwork[qb]
                P.op("dve", lambda e, iv=iv, m1=m1: e.max(out=m1.t[:, :], in_=iv), reads=[self.imp2], writes=[m1])
                P.op("dve", lambda e, iv=iv, m1=m1, wk=wk: e.match_replace(out=wk.t[:, :], in_to_replace=m1.t[:, :], in_values=iv, imm_value=-3.0e38),
                     reads=[self.imp2, m1], writes=[wk])
                P.op("dve", lambda e, m2=m2, wk=wk: e.max(out=m2.t[:, :], in_=wk.t[:, :]), reads=[wk], writes=[m2])
                P.op("dve", lambda e, iv=iv, qb=qb, m2=m2: e.tensor_scalar(out=self.notsel.t[:, qb * 32:(qb + 1) * 32], in0=iv,
                                                                          scalar1=m2.t[:, 7:8], scalar2=None, op0=ALU.is_lt),
                     reads=[self.imp2, m2], writes=[self.notsel])

            def sel_transposes():
                for qb in range(4):
                    P.op("pe", lambda e, qb=qb: e.transpose(bX.t[0:32, qb * 128:(qb + 1) * 128], self.notsel.t[:, qb * 32:(qb + 1) * 32], identf),
                         reads=[self.notsel, self.cf32], writes=[bX])
                P.op("act", lambda e: e.activation(out=self.notselT.t[0:32, :], in_=bX.t[0:32, :], func=AF.Copy),
                     reads=[bX], writes=[self.notselT])

            items = []
            for br in (2, 1):
                for r in range(4):
                    kts = list(range(k0 // 128, 4 * qt + 4)) if br == 2 else list(range(0, 4 * qt + 4))
                    for n_i, kt in enumerate(kts):
                        items.append(dict(br=br, r=r, h=4 * g + r, kt=kt, n_i=n_i, last=(n_i == len(kts) - 1)))

            def emit_S(it):
                br, h, kt = it["br"], it["h"], it["kt"]
                if br == 1 and it["r"] == 0 and it["n_i"] == 0:
                    sel_transposes()
                i = kt - 4 * qt
                if i >= 0:
                    c0, c1 = 128 * i, T
                elif br == 2:
                    c0, c1 = 0, min(T, 128 * (i + 5))
                else:
                    c0, c1 = 0, T
                it["cols"] = (c0, c1)
                bank = bS[st["s"] % 3]
                st["s"] += 1
                it["bank"] = bank
                if br == 1:
                    extra = [(cb.t[0:32, self.EXN + kt * 128:self.EXN + (kt + 1) * 128], self.notselT.t[0:32, c0:c1], [cb, self.notselT])]
                    if i >= 0:
                        extra.append((identb, cb.t[:, self.CB + i * T + c0:self.CB + i * T + c1], [cb]))
                    kap, kbuf = ksT.t[:, kt * 128:(kt + 1) * 128], ksT
                else:
                    col = self.CB + i * T if i >= 0 else self.WB + (i + 4) * T
                    extra = [(identb, cb.t[:, col + c0:col + c1], [cb])]
                    kap, kbuf = kwT.t[:, kt * 128 - k0:(kt + 1) * 128 - k0], kwT
                self.mm(bank, bank.t[:, c0:c1], kap, self.qrot[h].t[:, c0:c1], True, False, [kbuf, self.qrot[h]])
                for xi, (l_ap, r_ap, rds) in enumerate(extra):
                    self.mm(bank, bank.t[:, c0:c1], l_ap, r_ap, False, xi == len(extra) - 1, rds)

            def emit_PV(it):
                br, r, h, kt, n_i, last = it["br"], it["r"], it["h"], it["kt"], it["n_i"], it["last"]
                bank = it["bank"]
                c0, c1 = it["cols"]
                if n_i == 0:
                    st["cur"] = bOsets[st["hd"] % 2]
                    st["hd"] += 1
                    zero_init(st["cur"], h)
                pair = st["cur"]
                Eb = Ebufs[st["e"] % 3]
                st["e"] += 1
                P.op("act", lambda e: e.activation(out=Eb.t[:, c0:c1], in_=bank.t[:, c0:c1], func=AF.Exp, scale=SCALE),
                     reads=[bank], writes=[Eb])
                if br == 1:
                    vap, vbuf = vs.t[:, kt, 0:129], vs
                else:
                    vap, vbuf = vw.t[:, kt - k0 // 128, 0:129], vw
                pv(pair, Eb, 128, vap, vbuf, range(c0 // 128, c1 // 128))
                if last:
                    epilogue(pair, h, br, r, False, False, br == 1)
            LAG = 2
            for idx in range(len(items) + LAG):
                if idx < len(items):
                    emit_S(items[idx])
                if idx >= LAG:
                    emit_PV(items[idx - LAG])
        wo = self.slabs[f"wo{j}"]

        def evac(ci, bank):
            return self.evac_with_stats(bank, ci, None, ci == 0, ci == KC - 1)
        self.proj_fm(self.xn, [[wo[(0, ng)]] for ng in range(D // SLABN)], 1, evac)
        self.resid_add_norm(V["ln_mix_post"] + layer * KC)
        P.fence(self.attn_bufs, self.hidden)

    def load_h(self, src_buf, src_ap, kcs=range(KC)):
        for kc in kcs:
            self.P.dma("sp", lambda e, kc=kc: e.dma_start(out=self.h[kc].t, in_=src_ap[:, kc, :]), self.h[kc],
                       reads=[src_buf], writes=[self.h[kc]])

    def store_h(self, dst_buf, dst_ap, nxt=None):
        LAGH = 3
        for kc in range(KC + LAGH):
            if kc < KC:
                self.P.dma("act", lambda e, kc=kc: e.dma_start(out=dst_ap[:, kc, :], in_=self.h[kc].t), self.h[kc],
                           reads=[self.h[kc]], writes=[dst_buf])
            if nxt is not None and kc >= LAGH:
                self.load_h(nxt[0], nxt[1], [kc - LAGH])

    def build(self):
        P = self.P
        nc = self.nc
        V = {}
        col = 0
        for nm in ["ln_mix_pre", "ln_mix_post", "ln_mlp_pre", "ln_mlp_post"]:
            V[nm] = col
            col += 4 * KC
        for nm, n in [("conv_b_pw1", 2 * 2 * KC), ("conv_b_dw", 2 * KC), ("conv_ln_g", 2 * KC),
                      ("conv_ln_b", 2 * KC), ("conv_b_pw2", 2 * KC), ("ln_kv", KC)]:
            V[nm] = col
            col += n
        self.vcol = V
        Builder.NVEC = col
        self.declare()
        self.vecs = P.sbuf("vecs", [128, col], F32)
        self.ones_bf = P.sbuf("ones_bf", [128, 128], BF16)
        self.epsb = P.sbuf("epsb", [128, 4], F32)
        self.zeros_bf = P.sbuf("zeros_bf", [128, 128], BF16)
        self.pace = P.sbuf("pace", [128, 2], F32)
        self.hall = P.sbuf("hall", [128, KC, T], F32)
        self.h = [Buf(f"h{kc}", self.hall.t[:, kc, :]) for kc in range(KC)]
        self.xn = [P.sbuf(f"xn{kc}", [128, T], BF16) for kc in range(KC)]
        self.tmp32 = [P.sbuf(f"tmp32_{kc}", [128, T], F32) for kc in range(KC)]
        self.slots = [P.sbuf(f"slot{i}", [128, KC, SLABN], BF16) for i in range(3)]
        self.slot_rr = 0
        self.rstd = P.sbuf("rstd", [128, T], F32)
        self.rstd2 = P.sbuf("rstd2", [128, T], F32)
        self.sqb = [P.sbuf(f"sqb{i}", [128, T], BF16) for i in range(2)]
        self.ub = [P.sbuf(f"ub{i}", [128, T], F32) for i in range(2)]
        self.relub = [P.sbuf(f"relub{i}", [128, T], F32) for i in range(2)]
        regA = P.sb_off
        self.wdw = [P.sbuf(f"wdw{l}", [128, KC, CW], F32) for l in range(2)]
        self.ybf = [P.sbuf(f"ybf{i}", [128, T], BF16) for i in range(2)]
        self.sig = [P.sbuf(f"sig{i}", [128, T], F32) for i in range(2)]
        self.glu = [P.sbuf(f"glu{i}", [128, T + HALO], BF16) for i in range(2)]
        self.halo = P.sbuf("halo", [128, KC, HALO], BF16)
        self.halo_b = [Buf(f"halo{c}", self.halo.t[:, c, :]) for c in range(KC)]
        regA_conv_end = P.sb_off
        P.sb_off = max(P.sb_off, regA + 15744)
        regA_end = P.sb_off
        o = [regA]

        def A(name, shape, dtype):
            b = P.sbuf(name, shape, dtype, off=o[0])
            o[0] += (int(np.prod(shape[1:])) * (4 if dtype == F32 else 2) + 63) // 64 * 64
            return b
        self.cbf = A("cbf", [128, self.NCONST_BF], BF16)
        self.cf32 = A("cf32", [128, self.NCONST_F32], F32)
        self.tkb = A("tkb", [128, 128], F32)
        self.notselT = A("notselT", [128, T], BF16)
        assert o[0] <= regA_end, (o[0], regA_end)
        self.regA_conv = self.wdw + self.ybf + self.sig + self.glu + [self.halo] + self.halo_b
        self.regA_attn = [self.cbf, self.cf32, self.tkb, self.notselT]
        regB = P.sb_off
        self.cmpmask = P.sbuf("cmpmask", [128, T], BF16)
        self.kcmpT = P.sbuf("kcmpT", [128, 8, 128], BF16)
        self.vcmp = P.sbuf("vcmp", [128, 8, 129], BF16)
        self.kcmpT_b = [Buf(f"kcmpT{i}", self.kcmpT.t[:, i, :]) for i in range(8)]
        self.vcmp_b = [Buf(f"vcmp{i}", self.vcmp.t[:, i, :]) for i in range(8)]
        self.imp2 = P.sbuf("imp2", [128, 128], F32)
        self.notsel = P.sbuf("notsel", [128, 128], F32)
        tkw = P.sbuf("tkw", [128, 64], F32)
        self.tkwork = [Buf(f"tkwork{i}", tkw.t[:, (i % 2) * 32:(i % 2) * 32 + 32]) for i in range(2)] * 2
        m12 = P.sbuf("m12", [128, 64], F32)
        self.m1 = [Buf(f"m1_{i}", m12.t[:, i * 8:(i + 1) * 8]) for i in range(4)]
        self.m2 = [Buf(f"m2_{i}", m12.t[:, 32 + i * 8:32 + (i + 1) * 8]) for i in range(4)]
        self.biash = P.sbuf("biash", [128, 8], F32)
        self.eb2 = P.sbuf("eb2", [128, T], BF16)
        regB_end = P.sb_off
        self.regB_attn = [self.cmpmask, self.kcmpT, self.vcmp, self.imp2, self.notsel, tkw, m12, self.biash, self.eb2] \
            + self.kcmpT_b + self.vcmp_b + self.tkwork[0:2] + self.m1 + self.m2
        o[0] = regB
        self.identc = A("identc", [128, 128], BF16)
        NB1 = 11
        d0 = A("diag0", [128, NPE, 128], BF16)
        d1a = A("diag1a", [128, NB1, 128], BF16)
        assert o[0] <= regB_end, (o[0], regB_end)
        o[0] = regA_conv_end
        d1b = A("diag1b", [128, NPE - NB1, 128], BF16)
        assert o[0] <= regA_end, (o[0], regA_end)
        self.diag = [[(d0, k) for k in range(NPE)],
                     [(d1a, k) if k < NB1 else (d1b, k - NB1) for k in range(NPE)]]
        self.regB_conv = [self.identc, d0, d1a]
        self.regA_conv = self.regA_conv + [d1b]
        hid_off = P.sb_off
        self.hidden = [P.sbuf(f"hid{i}", [128, T], BF16) for i in range(FC)]
        self.kst = [P.sbuf(f"kst{i}", [128, T], BF16, off=hid_off + i * 1024) for i in range(2)]
        self.vst = [P.sbuf(f"vst{i}", [128, 4, 256], BF16, off=hid_off + 2048 + i * 2048) for i in range(2)]
        self.kv_stage = self.kst + self.vst
        self.kcbuf = [P.sbuf(f"kcbuf{i}", [128, S], BF16, off=hid_off + i * 4096) for i in range(8)]
        self.hidc = [P.sbuf(f"hidc{i}", [128, 4, 128], BF16, off=hid_off + 32768 + i * 1024) for i in range(8)]
        self.peT32 = P.sbuf("peT32", [128, 64], F32, off=hid_off + 40960)
        self.peT16 = P.sbuf("peT16", [128, 64], BF16, off=hid_off + 40960 + 256)
        self.w2f = [P.sbuf(f"w2f{i}", [128, 4, 128], F32, off=hid_off + 41984 + i * 2048) for i in range(2)]
        self.w2b = [P.sbuf(f"w2b{i}", [128, 4, 128], BF16, off=hid_off + 46080 + i * 1024) for i in range(2)]
        self.cmp_bufs = self.kcbuf + self.hidc + [self.peT32, self.peT16] + self.w2f + self.w2b
        self.q16 = [P.sbuf(f"q16_{i}", [128, T], BF16, off=hid_off + i * 1024) for i in range(NH)]
        self.qrot = [P.sbuf(f"qrot_{i}", [128, T], BF16, off=hid_off + 16384 + i * 1024) for i in range(NH)]
        self.kvset = []
        for i in range(2):
            b0 = hid_off + 32768 + i * 12416
            self.kvset.append(dict(
                ksT=P.sbuf(f"ksT{i}", [128, S], BF16, off=b0),
                vs=P.sbuf(f"vs{i}", [128, 16, 129], BF16, off=b0 + 4096),
                kwT=P.sbuf(f"kwT{i}", [128, 1024], BF16, off=b0 + 8256),
                vw=P.sbuf(f"vw{i}", [128, 8, 129], BF16, off=b0 + 10304)))
        self.Ec = [P.sbuf(f"Ec{i}", [128, T], F32, off=hid_off + 57600 + i * 2048) for i in range(2)]
        self.attn_bufs = self.q16 + self.qrot + [v for d in self.kvset for v in d.values()] + self.Ec
        print("SBUF bytes/partition used:", P.sb_off, flush=True)
        assert P.sb_off <= 229344
        self.banks = [P.psum(f"bank{i}") for i in range(6)]
        self.bank_rr = 0
        self.stat_banks = [P.psum(f"sbank{i}") for i in range(2)]

        P.dma("sp", lambda e: e.dma_start(out=self.vecs.t[:, :], in_=self.ext["vecs"].t[:, :]), self.vecs,
              reads=[self.ext["vecs"]], writes=[self.vecs])
        for l in range(2):
            P.dma("sp", lambda e, l=l: e.dma_start(out=self.wdw[l].t[:, :, :], in_=self.ext["conv_w_dw"].t[l]),
                  self.wdw[l], reads=[self.ext["conv_w_dw"]], writes=[self.wdw[l]])
        P.op("pool", lambda e: e.memset(self.ones_bf.t[:, :], 1.0), writes=[self.ones_bf])
        P.dma("sp", lambda e: e.dma_start(out=self.identc.t[:, :], in_=self.ext["consts_bf"].t[:, self.IDB:self.IDB + 128]),
              self.identc, reads=[self.ext["consts_bf"]], writes=[self.identc])
        P.op("pool", lambda e: e.memset(self.epsb.t[:, 0:1], 2048.0 * EPS), writes=[self.epsb])
        P.op("pool", lambda e: e.memset(self.zeros_bf.t[:, :], 0.0), writes=[self.zeros_bf])
        P.op("pool", lambda e: e.memset(self.epsb.t[:, 1:2], EPS), writes=[self.epsb])
        P.op("pool", lambda e: e.memset(self.epsb.t[:, 2:3], 1e-30), writes=[self.epsb])
        P.op("pool", lambda e: e.memset(self.epsb.t[:, 3:4], 0.0), writes=[self.epsb])

        self.slabs = {}
        E = self.ext
        nl = self.n_layers
        for l in range(min(2, nl)):
            def scr(nm):
                return self.slabs[nm][(0, 0)][0]
            z = (l == 0)
            self.slabs[f"pw1_{l}"] = self.make_slabs(f"pw1_{l}", E["conv_w_pw1"], l, D, 2 * D, now=z)
            self.slabs[f"pw2_{l}"] = self.make_slabs(f"pw2_{l}", E["conv_w_pw2"], l, D, D, now=z, after=scr(f"pw1_{l}") if z else None)
            self.slabs[f"up{l}"] = self.make_slabs(f"up{l}", E["mlp_w_up"], l, D, DFF, now=z, after=scr(f"pw2_{l}") if z else None)
            self.slabs[f"down{l}"] = self.make_slabs(f"down{l}", E["mlp_w_down"], l, DFF, D, now=z, after=scr(f"up{l}") if z else None)
        n_l1 = len(self.pending_convs)
        if self.attn:
            self.slabs["wkv"] = self.make_slabs("wkv", E["w_kv"], None, D, 3072)
            self.slabs["cmp_k_w1"] = self.make_slabs("cmp_k_w1", E["cmp_k_w1"], None, 4096, 512)
            self.slabs["cmp_v_w1"] = self.make_slabs("cmp_v_w1", E["cmp_v_w1"], None, 4096, 512)
            for j in range(nl - 2):
                self.slabs[f"win{j}"] = self.make_slabs(f"win{j}", E["attn_w_in"], j, D, 2096)
                self.slabs[f"wo{j}"] = self.make_slabs(f"wo{j}", E["attn_w_o"], j, D, D)
                self.slabs[f"up{2 + j}"] = self.make_slabs(f"up{2 + j}", E["mlp_w_up"], 2 + j, D, DFF)
                self.slabs[f"down{2 + j}"] = self.make_slabs(f"down{2 + j}", E["mlp_w_down"], 2 + j, DFF, D)
        ntl = self.nseq * NT
        self.l1_left = n_l1
        per_tile = {0: (n_l1 + ntl - 1) // ntl + 1, 1: (len(self.pending_convs) - n_l1 + ntl - 1) // ntl + 1}

        def h_ap(buf, g, whole):
            if whole:
                return buf.t[:, g * T:(g + 1) * T].rearrange("(kc p) t -> p kc t", p=128)
            return buf.t[:, :].rearrange("(kc p) t -> p kc t", p=128)

        for layer in range(nl):
            if layer == 2:
                P.fence(self.regA_conv, self.regA_attn)
                P.fence(self.regB_conv, self.regB_attn)
                P.dma("sp", lambda e: e.dma_start(out=self.cbf.t[:, :], in_=E["consts_bf"].t[:, :]), self.cbf,
                      reads=[E["consts_bf"]], writes=[self.cbf])
                P.dma("sp", lambda e: e.dma_start(out=self.cf32.t[:, :], in_=E["consts_f32"].t[:, :]), self.cf32,
                      reads=[E["consts_f32"]], writes=[self.cf32])
                P.op("pool", lambda e: e.memset(self.kcmpT.t[:, :, :], 0.0), writes=[self.kcmpT] + self.kcmpT_b)
                P.op("pool", lambda e: e.memset(self.vcmp.t[:, :, :], 0.0), writes=[self.vcmp] + self.vcmp_b)
                P.op("pool", lambda e: e.memset(self.vcmp.t[:, :, 128:129], 1.0), writes=[self.vcmp] + self.vcmp_b)
                ntl_kv = self.nseq * NT
                self.load_h(self.hT[0], h_ap(self.hT[0], 0, False))
                for g in range(ntl_kv):
                    b, ti = divmod(g, NT)
                    self.kv_next = (self.hT[g + 1], h_ap(self.hT[g + 1], g + 1, False)) if g + 1 < ntl_kv else None
                    self.kv_phase(b, ti)
                self.compress_phase()
            def src_of(g):
                if layer == 0:
                    return (self.xT, h_ap(self.xT, g, True))
                return (self.hT[g], h_ap(self.hT[g], g, False))
            ntl_ = self.nseq * NT
            self.load_h(*src_of(0))
            for g in range(ntl_):
                b, ti = divmod(g, NT)
                if layer == 0:
                    k_ = min(per_tile[0], self.l1_left) if g < ntl_ - 1 else self.l1_left
                    self.flush_convs(k_, gate=(g > 0))
                    self.l1_left -= k_
                elif layer == 1:
                    self.flush_convs(per_tile[1] if g < ntl_ - 1 else len(self.pending_convs))
                if layer < 2:
                    self.conv_mixer(layer, ti == 0)
                else:
                    self.attn_mixer(layer - 2, b, ti)
                self.mlp(layer)
                nxt = src_of(g + 1) if g + 1 < ntl_ else None
                if layer == nl - 1:
                    self.store_h(self.outT, h_ap(self.outT, g, True), nxt)
                else:
                    self.store_h(self.hT[g], h_ap(self.hT[g], g, False), nxt)
        if self.dbg and self.attn:
            P.dma("sp", lambda e: e.dma_start(out=self.dbg_kcmp.t[:, :, :], in_=self.kcmpT.t[:, :, :]), self.kcmpT,
                  reads=[self.kcmpT] + self.kcmpT_b, writes=[self.dbg_kcmp])
            P.dma("sp", lambda e: e.dma_start(out=self.dbg_vcmp.t[:, :, :], in_=self.vcmp.t[:, :, :]), self.vcmp,
                  reads=[self.vcmp] + self.vcmp_b, writes=[self.dbg_vcmp])
        P.finish()
        return nc


def _cols(v):
    v = np.asarray(v, np.float32).reshape(-1, 128)
    return np.ascontiguousarray(v.T)


def host_inputs(inputs, core, nseq):
    x = inputs["x"]
    xs = x[core * nseq:(core + 1) * nseq]
    xT = np.ascontiguousarray(xs.reshape(nseq * S, D).T)
    vec_list = []
    for nm in ["ln_mix_pre", "ln_mix_post", "ln_mlp_pre", "ln_mlp_post"]:
        for l in range(4):
            vec_list.append(_cols(inputs[nm][l]))
    for l in range(2):
        vec_list.append(_cols(inputs["conv_b_pw1"][l]))
    for nm in ["conv_b_dw", "conv_ln_g", "conv_ln_b", "conv_b_pw2"]:
        for l in range(2):
            vec_list.append(_cols(inputs[nm][l]))
    vec_list.append(_cols(inputs["ln_kv"]))
    vecs = np.ascontiguousarray(np.concatenate(vec_list, axis=1))
    wdw = inputs["conv_w_dw"]
    wdw_l = np.ascontiguousarray(wdw.reshape(2, CW, KC, 128).transpose(0, 3, 2, 1))
    m = {"xT": xT, "vecs": vecs, "conv_w_dw": wdw_l}
    for nm in ["conv_w_pw1", "conv_w_pw2", "mlp_w_up", "mlp_w_down", "w_kv", "attn_w_in", "attn_w_o",
               "cmp_k_w1", "cmp_v_w1", "cmp_k_w2", "cmp_v_w2"]:
        m[nm] = np.ascontiguousarray(inputs[nm])
    m["cmp_peT"] = np.ascontiguousarray(np.concatenate([inputs["cmp_pe_k"].T, inputs["cmp_pe_v"].T], axis=1).astype(np.float32))
    m.update(const_tables())
    return m


_CT = {}


def const_tables():
    if _CT:
        return _CT
    import ml_dtypes
    bf = ml_dtypes.bfloat16
    B = Builder
    cbf = np.zeros((128, B.NCONST_BF), np.float32)
    cbf[:, B.IDB:B.IDB + 128] = np.eye(128)
    kl = np.arange(128)[:, None]
    ql = np.arange(T)[None, :]
    for i in range(4):
        cbf[:, B.CB + i * T:B.CB + (i + 1) * T] = np.where(128 * i + kl <= ql, 0.0, NEGB)
    for i in range(-4, 0):
        cbf[:, B.WB + (i + 4) * T:B.WB + (i + 5) * T] = np.where(ql - 128 * i - kl < 512, 0.0, NEGB)
    for kt in range(16):
        for key in range(128):
            cbf[2 * kt + key // 64, B.EXN + kt * 128 + key] = NEGB
    cf = np.zeros((128, B.NCONST_F32), np.float32)
    cf[:, B.IDF:B.IDF + 128] = np.eye(128)
    n = np.arange(NCMP)[:, None]
    jj = np.arange(32)[None, :]
    cf[0:NCMP, B.OVL:B.OVL + 32] = ((16 * n < 64 * jj + 64) & (16 * n + 32 > 64 * jj)).astype(np.float32)
    for mcol in range(16):
        cf[mcol + 16, B.ROT + mcol] = -1.0
        cf[mcol, B.ROT + mcol + 16] = 1.0
    cf[:, B.ONEF:B.ONEF + 128] = 1.0
    pos = np.arange(S, dtype=np.float32)
    inv = (500000.0 ** (-np.arange(0, 32, 2, dtype=np.float32) / 32)).astype(np.float32)
    ang = pos[None, :] * inv[:, None]
    ropeC = np.ones((128, S), np.float32)
    ropeS = np.zeros((128, S), np.float32)
    ropeC[0:16] = np.cos(ang)
    ropeC[16:32] = np.cos(ang)
    ropeS[0:16] = np.sin(ang)
    ropeS[16:32] = np.sin(ang)
    t = np.arange(S)[:, None]
    blk = np.arange(32)[None, :]
    cur = t // 64
    forced = (blk == 0) | (blk == cur) | (blk == cur - 1)
    valid = blk * 64 <= t
    tk = np.where(forced, 1000.0, 0.0)
    tk = np.where(valid, tk, -1e30).astype(np.float32)
    cm = np.full((128, S), NEGB, np.float32)
    nn = np.arange(128)[:, None]
    cm[(16 * nn + 31 <= np.arange(S)[None, :]) & (nn < NCMP)] = 0.0
    _CT.update(consts_bf=cbf.astype(bf), consts_f32=cf, ropeC=ropeC, ropeS=ropeS, topk_bias=tk, cmpmask=cm.astype(bf))
    return _CT


_CACHE = {}


def kernel(**inputs):
    nseq = 2
    if "nc" not in _CACHE:
        _CACHE["nc"] = Builder(nseq=nseq, n_layers=4).build()
    nc = _CACHE["nc"]
    in_maps = [host_inputs(inputs, c, nseq) for c in range(8)]
    res = run_bass_kernel_spmd(nc, in_maps, core_ids=list(range(8)))
    outs = []
    for c in range(8):
        oT = res.results[c]["outT"]
        outs.append(np.ascontiguousarray(oT.T).reshape(nseq, S, D))
    return np.concatenate(outs, axis=0).astype(np.float32)
```

```python
from contextlib import ExitStack
import numpy as np
import concourse.bass as bass
import concourse.mybir as mybir
from concourse.bass_utils import run_bass_kernel_spmd

F32 = mybir.dt.float32
BF16 = mybir.dt.bfloat16
AF = mybir.ActivationFunctionType
ALU = mybir.AluOpType

D = 2048
KC = 16
S = 2048
T = 512
NT = S // T
DFF = 8192
FC = DFF // 128
CW = 31
HALO = CW - 1
EPS = 1e-6
NH = 16
NG = 4
NCMP = 127
SCALE = 128 ** -0.5
NEGB = -30000.0
SLABN = 256
NPE = 18
EPOCH = 30000
COMPUTE = ("pe", "act", "dve", "pool")
SQ2048 = float(np.sqrt(2048.0))


class Buf:
    __slots__ = ("name", "t", "w", "r", "sem", "dma_cnt", "id")
    _n = 0

    def __init__(self, name, t):
        self.name = name
        self.t = t
        self.w = {}
        self.r = {}
        self.sem = None
        self.dma_cnt = 0
        Buf._n += 1
        self.id = Buf._n

    def __getitem__(self, idx):
        return self.t[idx]


class Instr:
    __slots__ = ("fn", "deps", "me", "waits", "signal", "semval")

    def __init__(self, fn, deps, me):
        self.fn = fn
        self.deps = deps
        self.me = me
        self.waits = None
        self.signal = False
        self.semval = None


class Prog:
    def __init__(self, nc, es):
        self.nc = nc
        self.es = es
        self.streams = {e: [] for e in ("pe", "act", "dve", "pool", "sp")}
        self.ncomp = {e: 0 for e in COMPUTE}
        self.comp_list = {e: [] for e in COMPUTE}
        self.sb_off = 16512
        self.dma_bufs = {}

    def sbuf(self, name, shape, dtype, off=None):
        esz = 4 if dtype == F32 else 2
        per = int(np.prod(shape[1:])) * esz
        if off is None:
            off = self.sb_off
            self.sb_off += (per + 63) // 64 * 64
        t = self.nc.alloc_sbuf_tensor_at(name, list(shape), dtype, offset=off)
        return Buf(name, t)

    def psum(self, name, shape=(128, 512), dtype=F32):
        return Buf(name, self.nc.alloc_psum_tensor(name, list(shape), dtype))

    def dram(self, name, shape, dtype, kind="Internal"):
        return Buf(name, self.nc.dram_tensor(name, list(shape), dtype, kind=kind))

    @staticmethod
    def _deps(reads, writes):
        deps = {}
        for b in reads:
            for k, v in b.w.items():
                if deps.get(k, 0) < v:
                    deps[k] = v
        for b in writes:
            for k, v in b.w.items():
                if deps.get(k, 0) < v:
                    deps[k] = v
            for k, v in b.r.items():
                if deps.get(k, 0) < v:
                    deps[k] = v
        return deps

    @staticmethod
    def _mark(me_key, me_val, reads, writes):
        for b in writes:
            b.w = {me_key: me_val}
            b.r = {}
        for b in reads:
            if b.r.get(me_key, 0) < me_val:
                b.r[me_key] = me_val

    def op(self, eng, fn, reads=(), writes=()):
        deps = self._deps(reads, writes)
        self.ncomp[eng] += 1
        idx = self.ncomp[eng]
        ins = Instr(fn, deps, ("c", eng, idx))
        self.streams[eng].append(ins)
        self.comp_list[eng].append(ins)
        wset = set(id(b) for b in writes)
        self._mark(("c", eng), idx, [b for b in reads if id(b) not in wset], writes)

    def dma(self, q, fn, owner, reads=(), writes=(), extra=None):
        deps = self._deps(reads, writes)
        if extra:
            for k, v in extra.items():
                if deps.get(k, 0) < v:
                    deps[k] = v
        owner.dma_cnt += 1
        self.dma_bufs[owner.id] = owner
        ins = Instr(fn, deps, ("d", owner, owner.dma_cnt))
        self.streams[q].append(ins)
        key = ("d", owner.id)
        for b in writes:
            if b.t.__class__.__name__.startswith("DRam"):
                if b.w.get(key, 0) < owner.dma_cnt:
                    b.w[key] = owner.dma_cnt
                b.r = {}
            else:
                b.w = {key: owner.dma_cnt}
                b.r = {}
        for b in reads:
            if b.r.get(key, 0) < owner.dma_cnt:
                b.r[key] = owner.dma_cnt

    def fence(self, src, dst):
        acc = {}
        for b in src:
            for dct in (b.w, b.r):
                for k, v in dct.items():
                    if acc.get(k, 0) < v:
                        acc[k] = v
        for b in dst:
            for k, v in acc.items():
                if b.r.get(k, 0) < v:
                    b.r[k] = v

    def finish(self):
        nc = self.nc
        for sname, lst in self.streams.items():
            waited = {}
            for ins in lst:
                ws = []
                for k, v in ins.deps.items():
                    if k[0] == "c" and k[1] == "pe" and sname == "pe":
                        continue
                    if waited.get(k, 0) >= v:
                        continue
                    waited[k] = v
                    ws.append((k, v))
                    if k[0] == "c":
                        self.comp_list[k[1]][v - 1].signal = True
                ins.waits = ws
        self.eng_sems = {e: [] for e in COMPUTE}
        for e in COMPUTE:
            cnt = 0
            for ins in self.comp_list[e]:
                if ins.signal:
                    ep, val = divmod(cnt, EPOCH)
                    cnt += 1
                    ins.semval = (ep, val + 1)
            nep = (cnt + EPOCH - 1) // EPOCH if cnt else 0
            for i in range(max(nep, 1)):
                self.eng_sems[e].append(self.es.enter_context(nc.semaphore(f"s_{e}_{i}")))
        for b in self.dma_bufs.values():
            b.sem = self.es.enter_context(nc.semaphore(f"d_{b.name}_{b.id}"))
        nsig = {e: sum(1 for i in self.comp_list[e] if i.signal) for e in COMPUTE}
        print("instr counts", {k: len(v) for k, v in self.streams.items()}, "signals", nsig,
              "dma sems", len(self.dma_bufs), flush=True)

        def replay(sname):
            def run(eng):
                for ins in self.streams[sname]:
                    for k, v in ins.waits:
                        if k[0] == "c":
                            ep, val = self.comp_list[k[1]][v - 1].semval
                            eng.wait_ge(self.eng_sems[k[1]][ep], val)
                        else:
                            eng.wait_ge(self.dma_bufs[k[1]].sem, 16 * v)
                    bi = ins.fn(eng)
                    if ins.me[0] == "d":
                        bi.then_inc(ins.me[1].sem, 16)
                    elif ins.signal:
                        bi.then_inc(self.eng_sems[ins.me[1]][ins.semval[0]], 1)
                if sname == "sp":
                    for b in self.dma_bufs.values():
                        if b.dma_cnt:
                            eng.wait_ge(b.sem, 16 * b.dma_cnt)
            return run

        with nc.Block() as block:
            block.sync(replay("sp"))
            block.tensor(replay("pe"))
            block.scalar(replay("act"))
            block.vector(replay("dve"))
            block.gpsimd(replay("pool"))


class Builder:
    def __init__(self, nseq=2, n_layers=4, dbg=False):
        self.nseq = nseq
        self.n_layers = n_layers
        self.dbg = dbg
        self.attn = n_layers > 2
        self.ntok = nseq * S
        self.nc = bass.Bass("TRN2", target_bir_lowering=False)
        self.es = ExitStack()
        self.P = Prog(self.nc, self.es)
        self.cast_rr = 0
        self.rope_rr = 0
        self.pending_convs = []
        self.conv_seq = []

    def declare(self):
        P = self.P
        n = self.ntok
        self.xT = P.dram("xT", [D, n], F32, kind="ExternalInput")
        self.outT = P.dram("outT", [D, n], F32, kind="ExternalOutput")
        ext = {}

        def inp(name, shape, dtype=F32):
            ext[name] = P.dram(name, shape, dtype, kind="ExternalInput")
            return ext[name]
        self.ext = ext
        inp("vecs", [128, self.NVEC], F32)
        inp("conv_w_dw", [2, 128, KC, CW], F32)
        inp("conv_w_pw1", [2, D, 2 * D])
        inp("conv_w_pw2", [2, D, D])
        inp("mlp_w_up", [4, D, DFF])
        inp("mlp_w_down", [4, DFF, D])
        inp("consts_bf", [128, self.NCONST_BF], BF16)
        if self.attn:
            inp("w_kv", [D, 3072])
            inp("attn_w_in", [2, D, 2096])
            inp("attn_w_o", [2, D, D])
            inp("cmp_k_w1", [4096, 512])
            inp("cmp_v_w1", [4096, 512])
            inp("cmp_k_w2", [512, 128])
            inp("cmp_v_w2", [512, 128])
            inp("cmp_peT", [128, 64])
            inp("consts_f32", [128, self.NCONST_F32], F32)
            inp("ropeC", [128, S])
            inp("ropeS", [128, S])
            inp("topk_bias", [S, 32])
            inp("cmpmask", [128, S], BF16)
            knd = "ExternalOutput" if self.dbg else "Internal"
            self.kc_scr = [P.dram(f"kc_scr{b}", [8, 128, S], BF16, kind=knd) for b in range(self.nseq)]
            self.kT_scr = [P.dram(f"kT_scr{b}", [8, 128, S], BF16, kind=knd) for b in range(self.nseq)]
            self.v_scr = [P.dram(f"v_scr{b}", [8, S, 128], BF16, kind=knd) for b in range(self.nseq)]
            if self.dbg:
                self.dbg_kcmp = P.dram("dbg_kcmp", [128, 8, 128], BF16, kind="ExternalOutput")
                self.dbg_vcmp = P.dram("dbg_vcmp", [128, 8, 129], BF16, kind="ExternalOutput")
        self.hT = [P.dram(f"hT{i}", [D, T], F32) for i in range(self.nseq * NT)]

    VEC_NAMES = ["ln_mix_pre", "ln_mix_post", "ln_mlp_pre", "ln_mlp_post"]
    NVEC = 4 * 4 * KC + 2 * (2 * KC + 5 * KC) + KC
    IDB, CB, WB, EXN = 0, 128, 128 + 2048, 128 + 4096
    NCONST_BF = 128 + 4096 + 2048
    IDF, OVL, ROT, ONEF = 0, 128, 160, 288
    NCONST_F32 = 416

    def make_slabs(self, name, wbuf, lead, kdim, ncols, now=False, after=None):
        P = self.P
        nkg = kdim // 2048
        nng = (ncols + SLABN - 1) // SLABN
        scratch = P.dram(f"ws_{name}", [nkg * nng, 128, KC * SLABN], BF16)
        slabs = {}
        for kg in range(nkg):
            for ng in range(nng):
                w = min(SLABN, ncols - ng * SLABN)
                if lead is None:
                    src = wbuf.t[kg * 2048:(kg + 1) * 2048, ng * SLABN:ng * SLABN + w]
                else:
                    src = wbuf.t[lead, kg * 2048:(kg + 1) * 2048, ng * SLABN:ng * SLABN + w]
                src = src.rearrange("(kc p) n -> p kc n", p=128)
                sidx = kg * nng + ng
                dst = scratch.t[sidx].rearrange("p (kc n) -> p kc n", n=SLABN)[:, :, 0:w]

                def conv(src=src, dst=dst, scratch=scratch, wbuf=wbuf):
                    extra = None
                    if now and len(self.conv_seq) >= 2:
                        ob, oc = self.conv_seq[-2]
                        extra = {("d", ob.id): oc}
                    P.dma("pool", lambda e: e.dma_start(out=dst, in_=src), scratch, reads=[wbuf], writes=[scratch], extra=extra)
                    if now:
                        self.conv_seq.append((scratch, scratch.dma_cnt))
                if now:
                    conv()
                else:
                    self.pending_convs.append(conv)
                slabs[(kg, ng)] = (scratch, sidx, w)
        return slabs

    def flush_convs(self, n, gate=True):
        if gate and n > 0 and self.pending_convs:
            self.P.op("pool", lambda e: e.memset(self.pace.t[:, :], 0.0), reads=[self.xn[0]], writes=[self.pace])
        for _ in range(min(n, len(self.pending_convs))):
            self.pending_convs.pop(0)()

    def load_slab(self, slab):
        P = self.P
        scratch, sidx, w = slab
        sl = self.slots[self.slot_rr % len(self.slots)]
        self.slot_rr += 1
        src = scratch.t[sidx].rearrange("p (kc n) -> p kc n", n=SLABN)[:, :, 0:w]
        P.dma("sp", lambda e, sl=sl, src=src, w=w: e.dma_start(out=sl.t[:, :, 0:w], in_=src),
              sl, reads=[scratch], writes=[sl])
        return sl

    def mm(self, out_buf, out_ap, lhsT_ap, rhs_ap, start, stop, reads, sgc=False):
        if sgc:
            self.P.op("pe", lambda e: e.matmul(out_ap, lhsT_ap, rhs_ap, start=start, stop=stop, skip_group_check=True),
                      reads=reads, writes=[out_buf])
        else:
            self.P.op("pe", lambda e: e.matmul(out_ap, lhsT_ap, rhs_ap, start=start, stop=stop),
                      reads=reads, writes=[out_buf])

    def next_bank(self):
        b = self.banks[self.bank_rr % len(self.banks)]
        self.bank_rr += 1
        return b

    def proj_fm(self, x_chunks, slab_list, nk, consume):
        pend = []
        ci = 0
        for kgs in slab_list:
            w = kgs[0][2]
            nch = (w + 127) // 128
            banks = [self.next_bank() for _ in range(nch)]
            for kgi, slab in enumerate(kgs):
                sl = self.load_slab(slab)
                for j in range(nch):
                    m = min(128, w - j * 128)
                    for kc in range(KC):
                        xk = x_chunks[kgi * KC + kc]
                        self.mm(banks[j], banks[j].t[0:m, :], sl.t[:, kc, j * 128:j * 128 + m], xk.t[:, :],
                                start=(kgi == 0 and kc == 0), stop=(kgi == len(kgs) - 1 and kc == KC - 1),
                                reads=[sl, xk])
            for f in pend:
                f()
            pend = []
            for j in range(nch):
                f = consume(ci, banks[j])
                if f is not None:
                    pend.append(f)
                ci += 1
        for f in pend:
            f()

    def stats_add(self, bank, src_buf, first, last):
        self.mm(bank, bank.t[:, :], self.ones_bf.t[:, 0:128], src_buf.t[:, :], first, last, [self.ones_bf, src_buf])

    def sq_to(self, dst, src_buf, src_ap):
        self.P.op("act", lambda e: e.activation(out=dst.t[:, :], in_=src_ap, func=AF.Square),
                  reads=[src_buf], writes=[dst])

    def rstd_from(self, bank, dst, scale=1.0 / D):
        self.pow_act(bank, dst, scale, 1, -0.5)

    def pow_act(self, src, dst, scale, bias_col, power, parts=128):
        self.P.op("act", lambda e: e.activation(out=dst.t[0:parts, :], in_=src.t[0:parts, :], func=AF.Ln,
                                                bias=self.epsb.t[0:parts, bias_col:bias_col + 1], scale=scale),
                  reads=[src, self.epsb], writes=[dst])
        self.P.op("act", lambda e: e.activation(out=dst.t[0:parts, :], in_=dst.t[0:parts, :], func=AF.Exp, scale=power),
                  reads=[dst], writes=[dst])

    def vec(self, col):
        return self.vecs.t[:, col:col + 1]

    def rmsnorm_to_bf(self, src_chunks, gcol, dst_chunks):
        P = self.P
        bank = self.stat_banks[0]
        for kc in range(KC):
            sq = self.sqb[kc % 2]
            self.sq_to(sq, src_chunks[kc], src_chunks[kc].t[:, :])
            self.stats_add(bank, sq, kc == 0, kc == KC - 1)
        self.rstd_from(bank, self.rstd)
        for kc in range(KC):
            P.op("dve", lambda e, kc=kc: e.scalar_tensor_tensor(
                out=dst_chunks[kc].t[:, :], in0=src_chunks[kc].t[:, :], scalar=self.vec(gcol + kc),
                in1=self.rstd.t[:, :], op0=ALU.mult, op1=ALU.mult),
                reads=[src_chunks[kc], self.rstd, self.vecs], writes=[dst_chunks[kc]])

    def resid_add_norm(self, gcol):
        P = self.P
        self.rstd_from(self.stat_banks[1], self.rstd2)
        for kc in range(KC):
            u = self.ub[kc % 2]
            P.op("dve", lambda e, kc=kc, u=u: e.scalar_tensor_tensor(
                out=u.t[:, :], in0=self.tmp32[kc].t[:, :], scalar=self.vec(gcol + kc),
                in1=self.rstd2.t[:, :], op0=ALU.mult, op1=ALU.mult),
                reads=[self.tmp32[kc], self.rstd2, self.vecs], writes=[u])
            P.op("dve", lambda e, kc=kc, u=u: e.tensor_tensor(
                out=self.h[kc].t[:, :], in0=self.h[kc].t[:, :], in1=u.t[:, :], op=ALU.add),
                reads=[self.h[kc], u], writes=[self.h[kc]])

    def evac_with_stats(self, bank, kc, bias_col, first, last):
        P = self.P
        dst = self.tmp32[kc]
        if bias_col is None:
            P.op("act", lambda e: e.activation(out=dst.t[:, :], in_=bank.t[:, :], func=AF.Copy),
                 reads=[bank], writes=[dst])
        else:
            P.op("act", lambda e: e.activation(out=dst.t[:, :], in_=bank.t[:, :], func=AF.Identity,
                                               bias=self.vec(bias_col)),
                 reads=[bank, self.vecs], writes=[dst])
        sq = self.sqb[kc % 2]
        self.sq_to(sq, dst, dst.t[:, :])
        return lambda: self.stats_add(self.stat_banks[1], sq, first, last)

    def mlp(self, layer):
        P = self.P
        V = self.vcol
        self.rmsnorm_to_bf(self.h, V["ln_mlp_pre"] + layer * KC, self.xn)

        def relu2(ci, bank):
            r = self.relub[ci % 2]
            P.op("act", lambda e: e.activation(out=r.t[:, :], in_=bank.t[:, :], func=AF.Relu),
                 reads=[bank], writes=[r])
            P.op("act", lambda e: e.activation(out=self.hidden[ci].t[:, :], in_=r.t[:, :], func=AF.Square),
                 reads=[r], writes=[self.hidden[ci]])
        up = self.slabs[f"up{layer}"]
        self.proj_fm(self.xn, [[up[(0, ng)]] for ng in range(DFF // SLABN)], 1, relu2)
        dn = self.slabs[f"down{layer}"]

        def evac(ci, bank):
            return self.evac_with_stats(bank, ci, None, ci == 0, ci == KC - 1)
        self.proj_fm(self.hidden, [[dn[(kg, ng)] for kg in range(4)] for ng in range(D // SLABN)], 4, evac)
        self.resid_add_norm(V["ln_mlp_post"] + layer * KC)

    def conv_mixer(self, layer, first_tile):
        P = self.P
        V = self.vcol
        self.rmsnorm_to_bf(self.h, V["ln_mix_pre"] + layer * KC, self.xn)
        pw1 = self.slabs[f"pw1_{layer}"]
        order = []
        for j in range(8):
            order.append([pw1[(0, 8 + j)]])
            order.append([pw1[(0, j)]])
        bcol = V["conv_b_pw1"] + layer * 2 * KC
        wdw = self.wdw[layer]

        pair = {}

        def consume(ci, bank):
            grp, j = divmod(ci, 4)
            if j < 2:
                c = 2 * grp + j
                sg = self.sig[j]
                P.op("act", lambda e: e.activation(out=sg.t[:, :], in_=bank.t[:, :], func=AF.Sigmoid,
                                                   bias=self.vec(bcol + KC + c)),
                     reads=[bank, self.vecs], writes=[sg])
                return
            c = 2 * grp + (j - 2)
            sg = self.sig[j - 2]
            gl = self.glu[c % 2]
            glb = gl
            dg = self.diag[c % 2]
            if first_tile:
                P.op("dve", lambda e: e.memset(gl.t[:, 0:HALO], 0.0), reads=[], writes=[gl])
            else:
                P.op("dve", lambda e: e.tensor_copy(out=gl.t[:, 0:HALO], in_=self.halo.t[:, c, :]),
                     reads=[self.halo_b[c]], writes=[gl])
            P.op("dve", lambda e: e.scalar_tensor_tensor(out=gl.t[:, HALO:HALO + T], in0=bank.t[:, :],
                                                         scalar=self.vec(bcol + c), in1=sg.t[:, :],
                                                         op0=ALU.add, op1=ALU.mult),
                 reads=[bank, sg, self.vecs], writes=[gl])
            P.op("dve", lambda e: e.tensor_copy(out=self.halo.t[:, c, :], in_=gl.t[:, T:T + HALO]),
                 reads=[gl], writes=[self.halo_b[c]])
            for k in range(NPE):
                dbuf, di = dg[k]
                P.op("act", lambda e, k=k, dbuf=dbuf, di=di: e.activation(out=dbuf.t[:, di, :], in_=self.identc.t[:, :], func=AF.Copy,
                                                                          scale=wdw.t[:, c, k:k + 1]),
                     reads=[self.identc, wdw], writes=[dbuf])
            pair[j - 2] = (c, gl, glb, dg)
            if j < 3:
                return
            prs = [pair[0], pair[1]]
            for (cc_, gl_, glb_, dg_) in prs:
                P.op("dve", lambda e, cc_=cc_, gl_=gl_: e.tensor_scalar(
                    out=self.tmp32[cc_].t[:, :], in0=gl_.t[:, NPE:NPE + T], scalar1=wdw.t[:, cc_, NPE:NPE + 1],
                    scalar2=self.vec(V["conv_b_dw"] + layer * KC + cc_), op0=ALU.mult, op1=ALU.add),
                    reads=[gl_, wdw, self.vecs], writes=[self.tmp32[cc_]])
            for k in range(NPE + 1, CW):
                for (cc_, gl_, glb_, dg_) in prs:
                    P.op("dve", lambda e, k=k, cc_=cc_, gl_=gl_: e.scalar_tensor_tensor(
                        out=self.tmp32[cc_].t[:, :], in0=gl_.t[:, k:k + T], scalar=wdw.t[:, cc_, k:k + 1],
                        in1=self.tmp32[cc_].t[:, :], op0=ALU.mult, op1=ALU.add),
                        reads=[gl_, wdw, self.tmp32[cc_]], writes=[self.tmp32[cc_]])

            def later():
                bks = [self.next_bank(), self.next_bank()]
                for pi, (cc_, gl_, glb_, dg_) in enumerate(prs):
                    bk = bks[pi]
                    for k in range(NPE):
                        self.mm(bk, bk.t[:, :], dg_[k][0].t[:, dg_[k][1], :], glb_.t[:, k:k + T], k == 0, k == NPE - 1, [dg_[k][0], glb_])
                for pi, (cc_, gl_, glb_, dg_) in enumerate(prs):
                    bk = bks[pi]
                    P.op("dve", lambda e, cc_=cc_, bk=bk: e.tensor_tensor(
                        out=self.tmp32[cc_].t[:, :], in0=self.tmp32[cc_].t[:, :], in1=bk.t[:, :], op=ALU.add),
                        reads=[self.tmp32[cc_], bk], writes=[self.tmp32[cc_]])
                for pi, (cc_, gl_, glb_, dg_) in enumerate(prs):
                    y = self.tmp32[cc_]
                    sq = self.sqb[cc_ % 2]
                    self.sq_to(sq, y, y.t[:, :])
                    self.stats_add(self.stat_banks[1], sq, cc_ == 0, cc_ == KC - 1)
                    yb = self.ybf[cc_ % 2]
                    P.op("act", lambda e, y=y, yb=yb: e.activation(out=yb.t[:, :], in_=y.t[:, :], func=AF.Copy), reads=[y], writes=[yb])
                    self.stats_add(self.stat_banks[0], yb, cc_ == 0, cc_ == KC - 1)
            return later
        self.proj_fm(self.xn, order, 1, consume)
        mean = self.rstd
        rs = self.rstd2
        msq = self.ub[0]
        P.op("dve", lambda e: e.tensor_scalar(out=mean.t[:, :], in0=self.stat_banks[0].t[:, :], scalar1=1.0 / D,
                                              scalar2=None, op0=ALU.mult),
             reads=[self.stat_banks[0]], writes=[mean])
        P.op("dve", lambda e: e.tensor_tensor(out=msq.t[:, :], in0=mean.t[:, :], in1=mean.t[:, :], op=ALU.mult),
             reads=[mean], writes=[msq])
        P.op("dve", lambda e: e.scalar_tensor_tensor(out=msq.t[:, :], in0=self.stat_banks[1].t[:, :], scalar=1.0 / D,
                                                     in1=msq.t[:, :], op0=ALU.mult, op1=ALU.subtract),
             reads=[self.stat_banks[1], msq], writes=[msq])
        self.rstd_from(msq, rs, scale=1.0)
        for c in range(KC):
            y = self.tmp32[c]
            u = self.ub[c % 2]
            P.op("dve", lambda e, y=y, u=u: e.tensor_tensor(out=u.t[:, :], in0=y.t[:, :], in1=mean.t[:, :], op=ALU.subtract),
                 reads=[y, mean], writes=[u])
            P.op("dve", lambda e, u=u: e.tensor_tensor(out=u.t[:, :], in0=u.t[:, :], in1=rs.t[:, :], op=ALU.mult),
                 reads=[u, rs], writes=[u])
            P.op("act", lambda e, u=u, c=c: e.activation(out=self.xn[c].t[:, :], in_=u.t[:, :], func=AF.Silu,
                                                         scale=self.vec(V["conv_ln_g"] + layer * KC + c),
                                                         bias=self.vec(V["conv_ln_b"] + layer * KC + c)),
                 reads=[u, self.vecs], writes=[self.xn[c]])
        pw2 = self.slabs[f"pw2_{layer}"]

        def evac(ci, bank):
            return self.evac_with_stats(bank, ci, V["conv_b_pw2"] + layer * KC + ci, ci == 0, ci == KC - 1)
        self.proj_fm(self.xn, [[pw2[(0, ng)]] for ng in range(D // SLABN)], 1, evac)
        self.resid_add_norm(V["ln_mix_post"] + layer * KC)


    def rope(self, bank, dst16, plain16=None):
        P = self.P
        xs = self.relub[self.rope_rr % 2]
        self.rope_rr += 1
        ropeC, ropeS = self.tmp32[4], self.tmp32[5]
        bankR = self.stat_banks[1]
        P.op("act", lambda e: e.activation(out=xs.t[:, :], in_=bank.t[:, :], func=AF.Copy), reads=[bank], writes=[xs])
        if plain16 is not None:
            P.op("act", lambda e: e.activation(out=plain16.t[:, :], in_=bank.t[:, :], func=AF.Copy), reads=[bank], writes=[plain16])

        def later():
            rot = self.cf32.t[:, self.ROT:self.ROT + 128]
            self.mm(bankR, bankR.t[:, :], rot, xs.t[:, :], True, True, [self.cf32, xs])
            t1, t2 = self.ub[0], self.ub[1]
            P.op("dve", lambda e: e.tensor_tensor(out=t1.t[:, :], in0=bankR.t[:, :], in1=ropeS.t[:, :], op=ALU.mult),
                 reads=[bankR, ropeS], writes=[t1])
            P.op("dve", lambda e: e.tensor_tensor(out=t2.t[:, :], in0=xs.t[:, :], in1=ropeC.t[:, :], op=ALU.mult),
                 reads=[xs, ropeC], writes=[t2])
            P.op("dve", lambda e: e.tensor_tensor(out=dst16.t[:, :], in0=t1.t[:, :], in1=t2.t[:, :], op=ALU.add),
                 reads=[t1, t2], writes=[dst16])
        return later

    def load_rope(self, ti):
        P = self.P
        for nm, dst in (("ropeC", self.tmp32[4]), ("ropeS", self.tmp32[5])):
            src = self.ext[nm]
            P.dma("sp", lambda e, dst=dst, src=src: e.dma_start(out=dst.t[:, :], in_=src.t[:, ti * T:(ti + 1) * T]),
                  dst, reads=[src], writes=[dst])

    def kv_phase(self, b, ti):
        P = self.P
        V = self.vcol
        P.fence(self.hidden, self.kv_stage)
        self.rmsnorm_to_bf(self.h, V["ln_kv"], self.xn)
        if getattr(self, "kv_next", None) is not None:
            self.load_h(*self.kv_next)
        self.load_rope(ti)
        wkv = self.slabs["wkv"]
        cnt = [0]

        def fm_consume(tensor_i):
            def consume(ci, bank):
                g = ci
                st = self.kst[cnt[0] % 2]
                cnt[0] += 1
                if tensor_i in (0, 1):
                    P.op("act", lambda e: e.activation(out=st.t[:, :], in_=bank.t[:, :], func=AF.Copy),
                         reads=[bank], writes=[st])
                    scr = self.kc_scr[b]
                    dst = scr.t[tensor_i * 4 + g][:, ti * T:(ti + 1) * T]
                    P.dma("sp", lambda e: e.dma_start(out=dst, in_=st.t[:, :]), st, reads=[st], writes=[scr])
                    return None
                lat = self.rope(bank, st)
                scr = self.kT_scr[b]
                which = 0 if tensor_i == 2 else 1
                dst = scr.t[which * 4 + g][:, ti * T:(ti + 1) * T]

                def later():
                    lat()
                    P.dma("sp", lambda e: e.dma_start(out=dst, in_=st.t[:, :]), st, reads=[st], writes=[scr])
                return later
            return consume
        for tensor_i in (0, 1, 2, 4):
            self.proj_fm(self.xn, [[wkv[(0, 2 * tensor_i)]], [wkv[(0, 2 * tensor_i + 1)]]], 1, fm_consume(tensor_i))
        for which, tensor_i in ((0, 3), (1, 5)):
            for half in range(2):
                sl = self.load_slab(wkv[(0, 2 * tensor_i + half)])
                st = self.vst[cnt[0] % 2]
                cnt[0] += 1
                for tb in range(4):
                    bank = self.next_bank()
                    for kc in range(KC):
                        self.mm(bank, bank.t[:, 0:256], self.xn[kc].t[:, tb * 128:(tb + 1) * 128], sl.t[:, kc, :],
                                kc == 0, kc == KC - 1, [sl, self.xn[kc]])
                    P.op("act", lambda e, bank=bank, tb=tb, st=st: e.activation(out=st.t[:, tb, :], in_=bank.t[:, 0:256], func=AF.Copy),
                         reads=[bank], writes=[st])
                scr = self.v_scr[b]
                for gg in range(2):
                    g = half * 2 + gg
                    dst = scr.t[which * 4 + g][ti * T:(ti + 1) * T, :].rearrange("(tb p) d -> p tb d", p=128)
                    P.dma("sp", lambda e, dst=dst, st=st, gg=gg: e.dma_start(out=dst, in_=st.t[:, :, gg * 128:(gg + 1) * 128]),
                          st, reads=[st], writes=[scr])
        P.fence(self.kv_stage, self.hidden)

    def compress_phase(self):
        P = self.P
        E = self.ext
        P.fence(self.hidden + self.kv_stage, self.cmp_bufs)
        P.dma("sp", lambda e: e.dma_start(out=self.peT32.t[:, :], in_=E["cmp_peT"].t[:, :]), self.peT32,
              reads=[E["cmp_peT"]], writes=[self.peT32])
        P.op("dve", lambda e: e.tensor_copy(out=self.peT16.t[:, :], in_=self.peT32.t[:, :]), reads=[self.peT32], writes=[self.peT16])
        for kv in range(2):
            self._compress_one(kv)
        P.fence(self.cmp_bufs, self.hidden + self.attn_bufs)

    def _compress_one(self, kv):
        P = self.P
        E = self.ext
        if True:
            w1 = self.slabs["cmp_k_w1" if kv == 0 else "cmp_v_w1"]
            w2src = E["cmp_k_w2" if kv == 0 else "cmp_v_w2"]
            w2f, w2b = self.w2f[kv], self.w2b[kv]
            P.dma("sp", lambda e, w2f=w2f, w2src=w2src: e.dma_start(out=w2f.t[:, :, :], in_=w2src.t[:, :].rearrange("(hc p) d -> p hc d", p=128)),
                  w2f, reads=[w2src], writes=[w2f])
            P.op("dve", lambda e, w2f=w2f, w2b=w2b: e.tensor_copy(out=w2b.t[:, :, :], in_=w2f.t[:, :, :]), reads=[w2f], writes=[w2b])
            for b in range(self.nseq):
                for g in range(NG):
                    kb = self.kcbuf[b * 4 + g]
                    src = self.kc_scr[b]
                    P.dma("sp", lambda e, kb=kb, src=src, g=g: e.dma_start(out=kb.t[:, :], in_=src.t[kv * 4 + g]),
                          kb, reads=[src], writes=[kb])
            for ng in range(2):
                sls = [self.load_slab(w1[(kg, ng)]) for kg in range(2)]
                bankB = self.next_bank()
                for j in range(2):
                    for l in range(32):
                        self.mm(bankB, bankB.t[:, j:j + 1], sls[l // 16].t[:, l % 16, j * 128:(j + 1) * 128],
                                self.peT16.t[:, kv * 32 + l:kv * 32 + l + 1], l == 0, l == 31, [sls[l // 16], self.peT16])
                P.op("act", lambda e, bankB=bankB, ng=ng: e.activation(out=self.biash.t[:, kv * 4 + ng * 2:kv * 4 + ng * 2 + 2], in_=bankB.t[:, 0:2], func=AF.Copy),
                     reads=[bankB], writes=[self.biash])
                for b in range(self.nseq):
                    for g in range(NG):
                        kb = self.kcbuf[b * 4 + g]
                        hc_buf = self.hidc[b * 4 + g]
                        for j in range(2):
                            bank = self.next_bank()
                            for l in range(32):
                                self.mm(bank, bank.t[:, 0:NCMP], sls[l // 16].t[:, l % 16, j * 128:(j + 1) * 128],
                                        kb.t[:, l:l + 16 * (NCMP - 1) + 1:16], l == 0, l == 31, [sls[l // 16], kb])
                            hc = ng * 2 + j
                            P.op("act", lambda e, bank=bank, hc_buf=hc_buf, hc=hc: e.activation(
                                out=hc_buf.t[:, hc, 0:NCMP], in_=bank.t[:, 0:NCMP], func=AF.Silu,
                                bias=self.biash.t[:, kv * 4 + hc:kv * 4 + hc + 1]),
                                reads=[bank, self.biash], writes=[hc_buf])
            for b in range(self.nseq):
                for g in range(NG):
                    hc_buf = self.hidc[b * 4 + g]
                    bank = self.next_bank()
                    if kv == 0:
                        for hc in range(4):
                            self.mm(bank, bank.t[:, 0:NCMP], w2b.t[:, hc, :], hc_buf.t[:, hc, 0:NCMP], hc == 0, hc == 3, [w2b, hc_buf])
                        dstb = self.kcmpT_b[b * 4 + g]
                        P.op("act", lambda e, bank=bank, dstb=dstb: e.activation(out=dstb.t[:, 0:NCMP], in_=bank.t[:, 0:NCMP], func=AF.Copy),
                             reads=[bank], writes=[dstb])
                    else:
                        for hc in range(4):
                            self.mm(bank, bank.t[0:NCMP, 0:128], hc_buf.t[:, hc, 0:NCMP], w2b.t[:, hc, :], hc == 0, hc == 3, [w2b, hc_buf])
                        dstb = self.vcmp_b[b * 4 + g]
                        P.op("act", lambda e, bank=bank, dstb=dstb: e.activation(out=dstb.t[0:NCMP, 0:128], in_=bank.t[0:NCMP, 0:128], func=AF.Copy),
                             reads=[bank], writes=[dstb])

    def attn_mixer(self, j, b, qt):
        P = self.P
        V = self.vcol
        layer = 2 + j
        P.fence(self.hidden, self.attn_bufs)
        self.rmsnorm_to_bf(self.h, V["ln_mix_pre"] + layer * KC, self.xn)
        self.load_rope(qt)
        E = self.ext
        P.dma("sp", lambda e: e.dma_start(out=self.tkb.t[:, :].rearrange("p (qb j) -> p qb j", j=32),
                                           in_=E["topk_bias"].t[qt * T:(qt + 1) * T, :].rearrange("(qb p) j -> p qb j", p=128)),
              self.tkb, reads=[E["topk_bias"]], writes=[self.tkb])
        P.dma("sp", lambda e: e.dma_start(out=self.cmpmask.t[:, :], in_=E["cmpmask"].t[:, qt * T:(qt + 1) * T]),
              self.cmpmask, reads=[E["cmpmask"]], writes=[self.cmpmask])
        win = self.slabs[f"win{j}"]
        gt = self.tmp32[6]

        def qcons(ci, bank):
            return self.rope(bank, self.qrot[ci], self.q16[ci])
        self.proj_fm(self.xn, [[win[(0, ng)]] for ng in range(8)], 1, qcons)

        bS = self.banks[0:3]
        bOsets = [(self.banks[3], self.banks[4]), (self.banks[5], self.stat_banks[0])]
        bX = self.stat_banks[1]
        slg = self.load_slab(win[(0, 8)])
        for qb in range(4):
            for kc in range(KC):
                self.mm(bX, bX.t[:, qb * 48:(qb + 1) * 48], self.xn[kc].t[:, qb * 128:(qb + 1) * 128], slg.t[:, kc, 0:48],
                        kc == 0, kc == KC - 1, [slg, self.xn[kc]])
        P.op("act", lambda e: e.activation(out=gt.t[:, 0:192], in_=bX.t[:, 0:192], func=AF.Sigmoid), reads=[bX], writes=[gt])

        cb = self.cbf
        identb = cb.t[:, self.IDB:self.IDB + 128]
        identf = self.cf32.t[:, self.IDF:self.IDF + 128]
        rd4b, c4b = self.rstd, self.rstd2
        Ebufs = [self.sqb[0], self.sqb[1], self.eb2]
        nk = (qt + 1) * T
        k0 = max(0, qt * T - T)
        st = dict(s=0, e=0, hd=0, ep=0)

        def zero_init(pair, h):
            for bk in pair:
                self.mm(bk, bk.t[:, 0:258], self.zeros_bf.t[:, :], self.qrot[h].t[:, 0:258], True, True, [self.zeros_bf, self.qrot[h]], sgc=True)

        def pv(pair, Eb, parts, vap, vbuf, qbs):
            for qb in qbs:
                bk = pair[qb // 2]
                off = (qb % 2) * 129
                self.mm(bk, bk.t[:, off:off + 129], Eb.t[0:parts, qb * 128:(qb + 1) * 128], vap, False, True, [vbuf, Eb], sgc=True)

        def epilogue(pair, h, br, r, eps, first, final):
            k = st["ep"] % 64
            st["ep"] += 1
            rd4 = rd4b.t[:, k * 4:(k + 1) * 4]
            c4 = c4b.t[:, k * 4:(k + 1) * 4]
            for hi, bk in enumerate(pair):
                dv = bk.t[:, 128:258:129]
                if eps:
                    P.op("dve", lambda e, dv=dv, hi=hi: e.tensor_scalar(out=rd4[:, 2 * hi:2 * hi + 2], in0=dv, scalar1=1e-30, scalar2=None, op0=ALU.add),
                         reads=[bk], writes=[rd4b])
                    P.op("dve", lambda e, hi=hi: e.reciprocal(out=rd4[:, 2 * hi:2 * hi + 2], in_=rd4[:, 2 * hi:2 * hi + 2]), reads=[rd4b], writes=[rd4b])
                else:
                    P.op("dve", lambda e, dv=dv, hi=hi: e.reciprocal(out=rd4[:, 2 * hi:2 * hi + 2], in_=dv), reads=[bk], writes=[rd4b])
            row = h * 3 + br
            P.op("dve", lambda e: e.tensor_tensor(out=c4, in0=rd4, in1=gt.t[:, row:row + 145:48], op=ALU.mult),
                 reads=[rd4b, gt], writes=[c4b])
            acc = self.tmp32[r]
            for qb in range(4):
                bk = pair[qb // 2]
                off = (qb % 2) * 129
                if first:
                    P.op("dve", lambda e, bk=bk, off=off, qb=qb: e.tensor_scalar(
                        out=acc.t[:, qb * 128:(qb + 1) * 128], in0=bk.t[:, off:off + 128], scalar1=c4[:, qb:qb + 1], scalar2=None, op0=ALU.mult),
                        reads=[bk, c4b], writes=[acc])
                else:
                    P.op("dve", lambda e, bk=bk, off=off, qb=qb: e.scalar_tensor_tensor(
                        out=acc.t[:, qb * 128:(qb + 1) * 128], in0=bk.t[:, off:off + 128], scalar=c4[:, qb:qb + 1],
                        in1=acc.t[:, qb * 128:(qb + 1) * 128], op0=ALU.mult, op1=ALU.add),
                        reads=[bk, c4b, acc], writes=[acc])
            if final:
                for qb in range(4):
                    P.op("pe", lambda e, qb=qb: e.transpose(bX.t[:, qb * 128:(qb + 1) * 128], acc.t[:, qb * 128:(qb + 1) * 128], identf),
                         reads=[acc, self.cf32], writes=[bX])
                P.op("act", lambda e: e.activation(out=self.xn[h].t[:, :], in_=bX.t[:, :], func=AF.Copy), reads=[bX], writes=[self.xn[h]])
            return rd4

        for g in range(NG):
            kvs = self.kvset[g % 2]
            ksT, vs, kwT, vw = kvs["ksT"], kvs["vs"], kvs["kwT"], kvs["vw"]
            kscr, vscr = self.kT_scr[b], self.v_scr[b]
            P.dma("sp", lambda e, ksT=ksT, g=g: e.dma_start(out=ksT.t[:, 0:nk], in_=kscr.t[g][:, 0:nk]),
                  ksT, reads=[kscr], writes=[ksT])
            P.dma("sp", lambda e, vs=vs, g=g: e.dma_start(out=vs.t[:, 0:nk // 128, 0:128],
                                                           in_=vscr.t[g][0:nk, :].rearrange("(kt p) d -> p kt d", p=128)),
                  vs, reads=[vscr], writes=[vs])
            P.op("pool", lambda e, vs=vs: e.memset(vs.t[:, 0:nk // 128, 128:129], 1.0), writes=[vs])
            P.dma("sp", lambda e, kwT=kwT, g=g: e.dma_start(out=kwT.t[:, 0:nk - k0], in_=kscr.t[4 + g][:, k0:nk]),
                  kwT, reads=[kscr], writes=[kwT])
            P.dma("sp", lambda e, vw=vw, g=g: e.dma_start(out=vw.t[:, 0:(nk - k0) // 128, 0:128],
                                                           in_=vscr.t[4 + g][k0:nk, :].rearrange("(kt p) d -> p kt d", p=128)),
                  vw, reads=[vscr], writes=[vw])
            P.op("pool", lambda e, vw=vw: e.memset(vw.t[:, 0:(nk - k0) // 128, 128:129], 1.0), writes=[vw])
            kc_b = self.kcmpT_b[b * 4 + g]
            vc_b = self.vcmp_b[b * 4 + g]
            ovl = self.cf32.t[0:NCMP, self.OVL:self.OVL + 32]

            def cmp_S(r):
                h = 4 * g + r
                bank = bS[st["s"] % 3]
                st["s"] += 1
                self.mm(bank, bank.t[0:NCMP, :], kc_b.t[:, 0:NCMP], self.q16[h].t[:, :], True, False, [kc_b, self.q16[h]])
                self.mm(bank, bank.t[0:NCMP, :], cb.t[0:NCMP, self.IDB:self.IDB + NCMP], self.cmpmask.t[0:NCMP, :], False, True,
                        [cb, self.cmpmask])
                return bank

            def cmp_rest(r, bank):
                h = 4 * g + r
                Eb = Ebufs[st["e"] % 3]
                st["e"] += 1
                Ec32 = self.Ec[r % 2]
                P.op("act", lambda e: e.activation(out=Eb.t[0:NCMP, :], in_=bank.t[0:NCMP, :], func=AF.Exp, scale=SCALE),
                     reads=[bank], writes=[Eb])
                P.op("act", lambda e: e.activation(out=Ec32.t[0:NCMP, :], in_=bank.t[0:NCMP, :], func=AF.Exp, scale=SCALE),
                     reads=[bank], writes=[Ec32])
                pair = bOsets[st["hd"] % 2]
                st["hd"] += 1
                zero_init(pair, h)
                pv(pair, Eb, NCMP, vc_b.t[0:NCMP, 0:129], vc_b, range(4))
                for qb in range(4):
                    c_ = r * 128 + qb * 32
                    self.mm(bX, bX.t[:, c_:c_ + 32], Ec32.t[0:NCMP, qb * 128:(qb + 1) * 128], ovl, True, True, [Ec32, self.cf32], sgc=True)
                rd4 = epilogue(pair, h, 0, r, True, True, False)
                for qb in range(4):
                    c_ = r * 128 + qb * 32
                    if r == 0:
                        P.op("dve", lambda e, qb=qb, c_=c_: e.tensor_scalar(out=self.imp2.t[:, qb * 32:(qb + 1) * 32], in0=bX.t[:, c_:c_ + 32],
                                                                          scalar1=rd4[:, qb:qb + 1], scalar2=None, op0=ALU.mult),
                             reads=[bX, rd4b], writes=[self.imp2])
                    else:
                        P.op("dve", lambda e, qb=qb, c_=c_: e.scalar_tensor_tensor(out=self.imp2.t[:, qb * 32:(qb + 1) * 32], in0=bX.t[:, c_:c_ + 32],
                                                                                 scalar=rd4[:, qb:qb + 1], in1=self.imp2.t[:, qb * 32:(qb + 1) * 32],
                                                                                 op0=ALU.mult, op1=ALU.add),
                             reads=[bX, rd4b, self.imp2], writes=[self.imp2])
            cbanks = {0: cmp_S(0)}
            for r in range(4):
                if r + 1 < 4:
                    cbanks[r + 1] = cmp_S(r + 1)
                cmp_rest(r, cbanks[r])
            P.op("dve", lambda e: e.tensor_tensor(out=self.imp2.t[:, :], in0=self.imp2.t[:, :], in1=self.tkb.t[:, :], op=ALU.add),
                 reads=[self.imp2, self.tkb], writes=[self.imp2])
            for qb in range(4):
                iv = self.imp2.t[:, qb * 32:(qb + 1) * 32]
                m1, m2, wk = self.m1[qb], self.m2[qb], self.tkwork[qb]
                P.op("dve", lambda e, iv=iv, m1=m1: e.max(out=m1.t[:, :], in_=iv), reads=[self.imp2], writes=[m1])
                P.op("dve", lambda e, iv=iv, m1=m1, wk=wk: e.match_replace(out=wk.t[:, :], in_to_replace=m1.t[:, :], in_values=iv, imm_value=-3.0e38),
                     reads=[self.imp2, m1], writes=[wk])
                P.op("dve", lambda e, m2=m2, wk=wk: e.max(out=m2.t[:, :], in_=wk.t[:, :]), reads=[wk], writes=[m2])
                P.op("dve", lambda e, iv=iv, qb=qb, m2=m2: e.tensor_scalar(out=self.notsel.t[:, qb * 32:(qb + 1) * 32], in0=iv,
                                                                          scalar1=m2.t[:, 7:8], scalar2=None, op0=ALU.is_lt),
                     reads=[self.imp2, m2], writes=[self.notsel])

            def sel_transposes():
                for qb in range(4):
                    P.op("pe", lambda e, qb=qb: e.transpose(bX.t[0:32, qb * 128:(qb + 1) * 128], self.notsel.t[:, qb * 32:(qb + 1) * 32], identf),
                         reads=[self.notsel, self.cf32], writes=[bX])
                P.op("act", lambda e: e.activation(out=self.notselT.t[0:32, :], in_=bX.t[0:32, :], func=AF.Copy),
                     reads=[bX], writes=[self.notselT])

            items = []
            for br in (2, 1):
                for r in range(4):
                    kts = list(range(k0 // 128, 4 * qt + 4)) if br == 2 else list(range(0, 4 * qt + 4))
                    for n_i, kt in enumerate(kts):
                        items.append(dict(br=br, r=r, h=4 * g + r, kt=kt, n_i=n_i, last=(n_i == len(kts) - 1)))

            def emit_S(it):
                br, h, kt = it["br"], it["h"], it["kt"]
                if br == 1 and it["r"] == 0 and it["n_i"] == 0:
                    sel_transposes()
                i = kt - 4 * qt
                if i >= 0:
                    c0, c1 = 128 * i, T
                elif br == 2:
                    c0, c1 = 0, min(T, 128 * (i + 5))
                else:
                    c0, c1 = 0, T
                it["cols"] = (c0, c1)
                bank = bS[st["s"] % 3]
                st["s"] += 1
                it["bank"] = bank
                if br == 1:
                    extra = [(cb.t[0:32, self.EXN + kt * 128:self.EXN + (kt + 1) * 128], self.notselT.t[0:32, c0:c1], [cb, self.notselT])]
                    if i >= 0:
                        extra.append((identb, cb.t[:, self.CB + i * T + c0:self.CB + i * T + c1], [cb]))
                    kap, kbuf = ksT.t[:, kt * 128:(kt + 1) * 128], ksT
                else:
                    col = self.CB + i * T if i >= 0 else self.WB + (i + 4) * T
                    extra = [(identb, cb.t[:, col + c0:col + c1], [cb])]
                    kap, kbuf = kwT.t[:, kt * 128 - k0:(kt + 1) * 128 - k0], kwT
                self.mm(bank, bank.t[:, c0:c1], kap, self.qrot[h].t[:, c0:c1], True, False, [kbuf, self.qrot[h]])
                for xi, (l_ap, r_ap, rds) in enumerate(extra):
                    self.mm(bank, bank.t[:, c0:c1], l_ap, r_ap, False, xi == len(extra) - 1, rds)

            def emit_PV(it):
                br, r, h, kt, n_i, last = it["br"], it["r"], it["h"], it["kt"], it["n_i"], it["last"]
                bank = it["bank"]
                c0, c1 = it["cols"]
                if n_i == 0:
                    st["cur"] = bOsets[st["hd"] % 2]
                    st["hd"] += 1
                    zero_init(st["cur"], h)
                pair = st["cur"]
                Eb = Ebufs[st["e"] % 3]
                st["e"] += 1
                P.op("act", lambda e: e.activation(out=Eb.t[:, c0:c1], in_=bank.t[:, c0:c1], func=AF.Exp, scale=SCALE),
                     reads=[bank], writes=[Eb])
                if br == 1:
                    vap, vbuf = vs.t[:, kt, 0:129], vs
                else:
                    vap, vbuf = vw.t[:, kt - k0 // 128, 0:129], vw
                pv(pair, Eb, 128, vap, vbuf, range(c0 // 128, c1 // 128))
                if last:
                    epilogue(pair, h, br, r, False, False, br == 1)
            LAG = 2
            for idx in range(len(items) + LAG):
                if idx < len(items):
                    emit_S(items[idx])
                if idx >= LAG:
                    emit_PV(items[idx - LAG])
        wo = self.slabs[f"wo{j}"]

        def evac(ci, bank):
            return self.evac_with_stats(bank, ci, None, ci == 0, ci == KC - 1)
        self.proj_fm(self.xn, [[wo[(0, ng)]] for ng in range(D // SLABN)], 1, evac)
        self.resid_add_norm(V["ln_mix_post"] + layer * KC)
        P.fence(self.attn_bufs, self.hidden)

    def load_h(self, src_buf, src_ap, kcs=range(KC), q="sp"):
        for kc in kcs:
            self.P.dma(q, lambda e, kc=kc: e.dma_start(out=self.h[kc].t, in_=src_ap[:, kc, :]), self.h[kc],
                       reads=[src_buf], writes=[self.h[kc]])

    def store_h(self, dst_buf, dst_ap, nxt=None):
        LAGH = 3
        for kc in range(KC + LAGH):
            if kc < KC:
                self.P.dma("act", lambda e, kc=kc: e.dma_start(out=dst_ap[:, kc, :], in_=self.h[kc].t), self.h[kc],
                           reads=[self.h[kc]], writes=[dst_buf])
            if nxt is not None and kc >= LAGH:
                self.load_h(nxt[0], nxt[1], [kc - LAGH], q="act")

    def build(self):
        P = self.P
        nc = self.nc
        V = {}
        col = 0
        for nm in ["ln_mix_pre", "ln_mix_post", "ln_mlp_pre", "ln_mlp_post"]:
            V[nm] = col
            col += 4 * KC
        for nm, n in [("conv_b_pw1", 2 * 2 * KC), ("conv_b_dw", 2 * KC), ("conv_ln_g", 2 * KC),
                      ("conv_ln_b", 2 * KC), ("conv_b_pw2", 2 * KC), ("ln_kv", KC)]:
            V[nm] = col
            col += n
        self.vcol = V
        Builder.NVEC = col
        self.declare()
        self.vecs = P.sbuf("vecs", [128, col], F32)
        self.ones_bf = P.sbuf("ones_bf", [128, 128], BF16)
        self.epsb = P.sbuf("epsb", [128, 4], F32)
        self.zeros_bf = P.sbuf("zeros_bf", [128, 128], BF16)
        self.pace = P.sbuf("pace", [128, 2], F32)
        self.hall = P.sbuf("hall", [128, KC, T], F32)
        self.h = [Buf(f"h{kc}", self.hall.t[:, kc, :]) for kc in range(KC)]
        self.xn = [P.sbuf(f"xn{kc}", [128, T], BF16) for kc in range(KC)]
        self.tmp32 = [P.sbuf(f"tmp32_{kc}", [128, T], F32) for kc in range(KC)]
        self.slots = [P.sbuf(f"slot{i}", [128, KC, SLABN], BF16) for i in range(3)]
        self.slot_rr = 0
        self.rstd = P.sbuf("rstd", [128, T], F32)
        self.rstd2 = P.sbuf("rstd2", [128, T], F32)
        self.sqb = [P.sbuf(f"sqb{i}", [128, T], BF16) for i in range(2)]
        self.ub = [P.sbuf(f"ub{i}", [128, T], F32) for i in range(2)]
        self.relub = [P.sbuf(f"relub{i}", [128, T], F32) for i in range(2)]
        regA = P.sb_off
        self.wdw = [P.sbuf(f"wdw{l}", [128, KC, CW], F32) for l in range(2)]
        self.ybf = [P.sbuf(f"ybf{i}", [128, T], BF16) for i in range(2)]
        self.sig = [P.sbuf(f"sig{i}", [128, T], F32) for i in range(2)]
        self.glu = [P.sbuf(f"glu{i}", [128, T + HALO], BF16) for i in range(2)]
        self.halo = P.sbuf("halo", [128, KC, HALO], BF16)
        self.halo_b = [Buf(f"halo{c}", self.halo.t[:, c, :]) for c in range(KC)]
        regA_conv_end = P.sb_off
        P.sb_off = max(P.sb_off, regA + 15744)
        regA_end = P.sb_off
        o = [regA]

        def A(name, shape, dtype):
            b = P.sbuf(name, shape, dtype, off=o[0])
            o[0] += (int(np.prod(shape[1:])) * (4 if dtype == F32 else 2) + 63) // 64 * 64
            return b
        self.cbf = A("cbf", [128, self.NCONST_BF], BF16)
        self.cf32 = A("cf32", [128, self.NCONST_F32], F32)
        self.tkb = A("tkb", [128, 128], F32)
        self.notselT = A("notselT", [128, T], BF16)
        assert o[0] <= regA_end, (o[0], regA_end)
        self.regA_conv = self.wdw + self.ybf + self.sig + self.glu + [self.halo] + self.halo_b
        self.regA_attn = [self.cbf, self.cf32, self.tkb, self.notselT]
        regB = P.sb_off
        self.cmpmask = P.sbuf("cmpmask", [128, T], BF16)
        self.kcmpT = P.sbuf("kcmpT", [128, 8, 128], BF16)
        self.vcmp = P.sbuf("vcmp", [128, 8, 129], BF16)
        self.kcmpT_b = [Buf(f"kcmpT{i}", self.kcmpT.t[:, i, :]) for i in range(8)]
        self.vcmp_b = [Buf(f"vcmp{i}", self.vcmp.t[:, i, :]) for i in range(8)]
        self.imp2 = P.sbuf("imp2", [128, 128], F32)
        self.notsel = P.sbuf("notsel", [128, 128], F32)
        tkw = P.sbuf("tkw", [128, 64], F32)
        self.tkwork = [Buf(f"tkwork{i}", tkw.t[:, (i % 2) * 32:(i % 2) * 32 + 32]) for i in range(2)] * 2
        m12 = P.sbuf("m12", [128, 64], F32)
        self.m1 = [Buf(f"m1_{i}", m12.t[:, i * 8:(i + 1) * 8]) for i in range(4)]
        self.m2 = [Buf(f"m2_{i}", m12.t[:, 32 + i * 8:32 + (i + 1) * 8]) for i in range(4)]
        self.biash = P.sbuf("biash", [128, 8], F32)
        self.eb2 = P.sbuf("eb2", [128, T], BF16)
        regB_end = P.sb_off
        self.regB_attn = [self.cmpmask, self.kcmpT, self.vcmp, self.imp2, self.notsel, tkw, m12, self.biash, self.eb2] \
            + self.kcmpT_b + self.vcmp_b + self.tkwork[0:2] + self.m1 + self.m2
        o[0] = regB
        self.identc = A("identc", [128, 128], BF16)
        NB1 = 11
        d0 = A("diag0", [128, NPE, 128], BF16)
        d1a = A("diag1a", [128, NB1, 128], BF16)
        assert o[0] <= regB_end, (o[0], regB_end)
        o[0] = regA_conv_end
        d1b = A("diag1b", [128, NPE - NB1, 128], BF16)
        assert o[0] <= regA_end, (o[0], regA_end)
        self.diag = [[(d0, k) for k in range(NPE)],
                     [(d1a, k) if k < NB1 else (d1b, k - NB1) for k in range(NPE)]]
        self.regB_conv = [self.identc, d0, d1a]
        self.regA_conv = self.regA_conv + [d1b]
        hid_off = P.sb_off
        self.hidden = [P.sbuf(f"hid{i}", [128, T], BF16) for i in range(FC)]
        self.kst = [P.sbuf(f"kst{i}", [128, T], BF16, off=hid_off + i * 1024) for i in range(2)]
        self.vst = [P.sbuf(f"vst{i}", [128, 4, 256], BF16, off=hid_off + 2048 + i * 2048) for i in range(2)]
        self.kv_stage = self.kst + self.vst
        self.kcbuf = [P.sbuf(f"kcbuf{i}", [128, S], BF16, off=hid_off + i * 4096) for i in range(8)]
        self.hidc = [P.sbuf(f"hidc{i}", [128, 4, 128], BF16, off=hid_off + 32768 + i * 1024) for i in range(8)]
        self.peT32 = P.sbuf("peT32", [128, 64], F32, off=hid_off + 40960)
        self.peT16 = P.sbuf("peT16", [128, 64], BF16, off=hid_off + 40960 + 256)
        self.w2f = [P.sbuf(f"w2f{i}", [128, 4, 128], F32, off=hid_off + 41984 + i * 2048) for i in range(2)]
        self.w2b = [P.sbuf(f"w2b{i}", [128, 4, 128], BF16, off=hid_off + 46080 + i * 1024) for i in range(2)]
        self.cmp_bufs = self.kcbuf + self.hidc + [self.peT32, self.peT16] + self.w2f + self.w2b
        self.q16 = [P.sbuf(f"q16_{i}", [128, T], BF16, off=hid_off + i * 1024) for i in range(NH)]
        self.qrot = [P.sbuf(f"qrot_{i}", [128, T], BF16, off=hid_off + 16384 + i * 1024) for i in range(NH)]
        self.kvset = []
        for i in range(2):
            b0 = hid_off + 32768 + i * 12416
            self.kvset.append(dict(
                ksT=P.sbuf(f"ksT{i}", [128, S], BF16, off=b0),
                vs=P.sbuf(f"vs{i}", [128, 16, 129], BF16, off=b0 + 4096),
                kwT=P.sbuf(f"kwT{i}", [128, 1024], BF16, off=b0 + 8256),
                vw=P.sbuf(f"vw{i}", [128, 8, 129], BF16, off=b0 + 10304)))
        self.Ec = [P.sbuf(f"Ec{i}", [128, T], F32, off=hid_off + 57600 + i * 2048) for i in range(2)]
        self.attn_bufs = self.q16 + self.qrot + [v for d in self.kvset for v in d.values()] + self.Ec
        print("SBUF bytes/partition used:", P.sb_off, flush=True)
        assert P.sb_off <= 229344
        self.banks = [P.psum(f"bank{i}") for i in range(6)]
        self.bank_rr = 0
        self.stat_banks = [P.psum(f"sbank{i}") for i in range(2)]

        P.dma("sp", lambda e: e.dma_start(out=self.vecs.t[:, :], in_=self.ext["vecs"].t[:, :]), self.vecs,
              reads=[self.ext["vecs"]], writes=[self.vecs])
        for l in range(2):
            P.dma("sp", lambda e, l=l: e.dma_start(out=self.wdw[l].t[:, :, :], in_=self.ext["conv_w_dw"].t[l]),
                  self.wdw[l], reads=[self.ext["conv_w_dw"]], writes=[self.wdw[l]])
        P.op("pool", lambda e: e.memset(self.ones_bf.t[:, :], 1.0), writes=[self.ones_bf])
        P.dma("sp", lambda e: e.dma_start(out=self.identc.t[:, :], in_=self.ext["consts_bf"].t[:, self.IDB:self.IDB + 128]),
              self.identc, reads=[self.ext["consts_bf"]], writes=[self.identc])
        P.op("pool", lambda e: e.memset(self.epsb.t[:, 0:1], 2048.0 * EPS), writes=[self.epsb])
        P.op("pool", lambda e: e.memset(self.zeros_bf.t[:, :], 0.0), writes=[self.zeros_bf])
        P.op("pool", lambda e: e.memset(self.epsb.t[:, 1:2], EPS), writes=[self.epsb])
        P.op("pool", lambda e: e.memset(self.epsb.t[:, 2:3], 1e-30), writes=[self.epsb])
        P.op("pool", lambda e: e.memset(self.epsb.t[:, 3:4], 0.0), writes=[self.epsb])

        self.slabs = {}
        E = self.ext
        nl = self.n_layers
        for l in range(min(2, nl)):
            def scr(nm):
                return self.slabs[nm][(0, 0)][0]
            z = (l == 0)
            self.slabs[f"pw1_{l}"] = self.make_slabs(f"pw1_{l}", E["conv_w_pw1"], l, D, 2 * D, now=z)
            self.slabs[f"pw2_{l}"] = self.make_slabs(f"pw2_{l}", E["conv_w_pw2"], l, D, D, now=z, after=scr(f"pw1_{l}") if z else None)
            self.slabs[f"up{l}"] = self.make_slabs(f"up{l}", E["mlp_w_up"], l, D, DFF, now=z, after=scr(f"pw2_{l}") if z else None)
            self.slabs[f"down{l}"] = self.make_slabs(f"down{l}", E["mlp_w_down"], l, DFF, D, now=z, after=scr(f"up{l}") if z else None)
        n_l1 = len(self.pending_convs)
        if self.attn:
            self.slabs["wkv"] = self.make_slabs("wkv", E["w_kv"], None, D, 3072)
            self.slabs["cmp_k_w1"] = self.make_slabs("cmp_k_w1", E["cmp_k_w1"], None, 4096, 512)
            self.slabs["cmp_v_w1"] = self.make_slabs("cmp_v_w1", E["cmp_v_w1"], None, 4096, 512)
            for j in range(nl - 2):
                self.slabs[f"win{j}"] = self.make_slabs(f"win{j}", E["attn_w_in"], j, D, 2096)
                self.slabs[f"wo{j}"] = self.make_slabs(f"wo{j}", E["attn_w_o"], j, D, D)
                self.slabs[f"up{2 + j}"] = self.make_slabs(f"up{2 + j}", E["mlp_w_up"], 2 + j, D, DFF)
                self.slabs[f"down{2 + j}"] = self.make_slabs(f"down{2 + j}", E["mlp_w_down"], 2 + j, DFF, D)
        ntl = self.nseq * NT
        self.l1_left = n_l1
        per_tile = {0: (n_l1 + ntl - 1) // ntl + 1, 1: (len(self.pending_convs) - n_l1 + ntl - 1) // ntl + 1}

        def h_ap(buf, g, whole):
            if whole:
                return buf.t[:, g * T:(g + 1) * T].rearrange("(kc p) t -> p kc t", p=128)
            return buf.t[:, :].rearrange("(kc p) t -> p kc t", p=128)

        for layer in range(nl):
            if layer == 2:
                P.fence(self.regA_conv, self.regA_attn)
                P.fence(self.regB_conv, self.regB_attn)
                P.dma("sp", lambda e: e.dma_start(out=self.cbf.t[:, :], in_=E["consts_bf"].t[:, :]), self.cbf,
                      reads=[E["consts_bf"]], writes=[self.cbf])
                P.dma("sp", lambda e: e.dma_start(out=self.cf32.t[:, :], in_=E["consts_f32"].t[:, :]), self.cf32,
                      reads=[E["consts_f32"]], writes=[self.cf32])
                P.op("pool", lambda e: e.memset(self.kcmpT.t[:, :, :], 0.0), writes=[self.kcmpT] + self.kcmpT_b)
                P.op("pool", lambda e: e.memset(self.vcmp.t[:, :, :], 0.0), writes=[self.vcmp] + self.vcmp_b)
                P.op("pool", lambda e: e.memset(self.vcmp.t[:, :, 128:129], 1.0), writes=[self.vcmp] + self.vcmp_b)
                ntl_kv = self.nseq * NT
                self.load_h(self.hT[0], h_ap(self.hT[0], 0, False))
                for g in range(ntl_kv):
                    b, ti = divmod(g, NT)
                    self.kv_next = (self.hT[g + 1], h_ap(self.hT[g + 1], g + 1, False)) if g + 1 < ntl_kv else None
                    self.kv_phase(b, ti)
                self.compress_phase()
            def src_of(g):
                if layer == 0:
                    return (self.xT, h_ap(self.xT, g, True))
                return (self.hT[g], h_ap(self.hT[g], g, False))
            ntl_ = self.nseq * NT
            self.load_h(*src_of(0))
            for g in range(ntl_):
                b, ti = divmod(g, NT)
                if layer == 0:
                    k_ = min(per_tile[0], self.l1_left) if g < ntl_ - 1 else self.l1_left
                    self.flush_convs(k_, gate=(g > 0))
                    self.l1_left -= k_
                elif layer == 1:
                    self.flush_convs(per_tile[1] if g < ntl_ - 1 else len(self.pending_convs))
                if layer < 2:
                    self.conv_mixer(layer, ti == 0)
                else:
                    self.attn_mixer(layer - 2, b, ti)
                self.mlp(layer)
                nxt = src_of(g + 1) if g + 1 < ntl_ else None
                if layer == nl - 1:
                    self.store_h(self.outT, h_ap(self.outT, g, True), nxt)
                else:
                    self.store_h(self.hT[g], h_ap(self.hT[g], g, False), nxt)
        if self.dbg and self.attn:
            P.dma("sp", lambda e: e.dma_start(out=self.dbg_kcmp.t[:, :, :], in_=self.kcmpT.t[:, :, :]), self.kcmpT,
                  reads=[self.kcmpT] + self.kcmpT_b, writes=[self.dbg_kcmp])
            P.dma("sp", lambda e: e.dma_start(out=self.dbg_vcmp.t[:, :, :], in_=self.vcmp.t[:, :, :]), self.vcmp,
                  reads=[self.vcmp] + self.vcmp_b, writes=[self.dbg_vcmp])
        P.finish()
        return nc


def _cols(v):
    v = np.asarray(v, np.float32).reshape(-1, 128)
    return np.ascontiguousarray(v.T)


def host_inputs(inputs, core, nseq):
    x = inputs["x"]
    xs = x[core * nseq:(core + 1) * nseq]
    xT = np.ascontiguousarray(xs.reshape(nseq * S, D).T)
    vec_list = []
    for nm in ["ln_mix_pre", "ln_mix_post", "ln_mlp_pre", "ln_mlp_post"]:
        for l in range(4):
            vec_list.append(_cols(inputs[nm][l]))
    for l in range(2):
        vec_list.append(_cols(inputs["conv_b_pw1"][l]))
    for nm in ["conv_b_dw", "conv_ln_g", "conv_ln_b", "conv_b_pw2"]:
        for l in range(2):
            vec_list.append(_cols(inputs[nm][l]))
    vec_list.append(_cols(inputs["ln_kv"]))
    vecs = np.ascontiguousarray(np.concatenate(vec_list, axis=1))
    wdw = inputs["conv_w_dw"]
    wdw_l = np.ascontiguousarray(wdw.reshape(2, CW, KC, 128).transpose(0, 3, 2, 1))
    m = {"xT": xT, "vecs": vecs, "conv_w_dw": wdw_l}
    for nm in ["conv_w_pw1", "conv_w_pw2", "mlp_w_up", "mlp_w_down", "w_kv", "attn_w_in", "attn_w_o",
               "cmp_k_w1", "cmp_v_w1", "cmp_k_w2", "cmp_v_w2"]:
        m[nm] = np.ascontiguousarray(inputs[nm])
    m["cmp_peT"] = np.ascontiguousarray(np.concatenate([inputs["cmp_pe_k"].T, inputs["cmp_pe_v"].T], axis=1).astype(np.float32))
    m.update(const_tables())
    return m


_CT = {}


def const_tables():
    if _CT:
        return _CT
    import ml_dtypes
    bf = ml_dtypes.bfloat16
    B = Builder
    cbf = np.zeros((128, B.NCONST_BF), np.float32)
    cbf[:, B.IDB:B.IDB + 128] = np.eye(128)
    kl = np.arange(128)[:, None]
    ql = np.arange(T)[None, :]
    for i in range(4):
        cbf[:, B.CB + i * T:B.CB + (i + 1) * T] = np.where(128 * i + kl <= ql, 0.0, NEGB)
    for i in range(-4, 0):
        cbf[:, B.WB + (i + 4) * T:B.WB + (i + 5) * T] = np.where(ql - 128 * i - kl < 512, 0.0, NEGB)
    for kt in range(16):
        for key in range(128):
            cbf[2 * kt + key // 64, B.EXN + kt * 128 + key] = NEGB
    cf = np.zeros((128, B.NCONST_F32), np.float32)
    cf[:, B.IDF:B.IDF + 128] = np.eye(128)
    n = np.arange(NCMP)[:, None]
    jj = np.arange(32)[None, :]
    cf[0:NCMP, B.OVL:B.OVL + 32] = ((16 * n < 64 * jj + 64) & (16 * n + 32 > 64 * jj)).astype(np.float32)
    for mcol in range(16):
        cf[mcol + 16, B.ROT + mcol] = -1.0
        cf[mcol, B.ROT + mcol + 16] = 1.0
    cf[:, B.ONEF:B.ONEF + 128] = 1.0
    pos = np.arange(S, dtype=np.float32)
    inv = (500000.0 ** (-np.arange(0, 32, 2, dtype=np.float32) / 32)).astype(np.float32)
    ang = pos[None, :] * inv[:, None]
    ropeC = np.ones((128, S), np.float32)
    ropeS = np.zeros((128, S), np.float32)
    ropeC[0:16] = np.cos(ang)
    ropeC[16:32] = np.cos(ang)
    ropeS[0:16] = np.sin(ang)
    ropeS[16:32] = np.sin(ang)
    t = np.arange(S)[:, None]
    blk = np.arange(32)[None, :]
    cur = t // 64
    forced = (blk == 0) | (blk == cur) | (blk == cur - 1)
    valid = blk * 64 <= t
    tk = np.where(forced, 1000.0, 0.0)
    tk = np.where(valid, tk, -1e30).astype(np.float32)
    cm = np.full((128, S), NEGB, np.float32)
    nn = np.arange(128)[:, None]
    cm[(16 * nn + 31 <= np.arange(S)[None, :]) & (nn < NCMP)] = 0.0
    _CT.update(consts_bf=cbf.astype(bf), consts_f32=cf, ropeC=ropeC, ropeS=ropeS, topk_bias=tk, cmpmask=cm.astype(bf))
    return _CT


_CACHE = {}


def kernel(**inputs):
    nseq = 2
    if "nc" not in _CACHE:
        _CACHE["nc"] = Builder(nseq=nseq, n_layers=4).build()
    nc = _CACHE["nc"]
    in_maps = [host_inputs(inputs, c, nseq) for c in range(8)]
    res = run_bass_kernel_spmd(nc, in_maps, core_ids=list(range(8)))
    outs = []
    for c in range(8):
        oT = res.results[c]["outT"]
        outs.append(np.ascontiguousarray(oT.T).reshape(nseq, S, D))
    return np.concatenate(outs, axis=0).astype(np.float32)
```
